# Optimizing a Trainium2 kernel written in Bass

```python
import math
import numpy as np
import jax
import jax.numpy as jnp
from jax import lax

D_MODEL = 2048
BATCH = 1
SEQ = 16384
DEPTH = 2

HEAD_DIM = 128
MIX_HEADS = D_MODEL // HEAD_DIM
MIX_W = MIX_HEADS * HEAD_DIM
QBLOCK = 128
NORM_EPS = 1e-6
RET_HEADS = MIX_HEADS // 2
FOX_HEADS = MIX_HEADS - RET_HEADS
RET_W = RET_HEADS * HEAD_DIM
FOX_W = FOX_HEADS * HEAD_DIM
RET_CHUNK = 128
FOX_GATE_BIAS = 3.0
DIFF_HEADS = MIX_HEADS // 2
NSA_HEADS = MIX_HEADS - DIFF_HEADS
DIFF_SUB = HEAD_DIM // 2
DIFF_W = DIFF_HEADS * HEAD_DIM
NSA_W = NSA_HEADS * HEAD_DIM
NSA_KV_HEADS = 2
NSA_GROUP = NSA_HEADS // NSA_KV_HEADS
NSA_KV_W = NSA_KV_HEADS * HEAD_DIM
CMP_LEN = 32
CMP_STRIDE = 16
SLC_LEN = 64
SLC_TOPN = 16
WINDOW = 512
SEL_BIG = 1e9
T5_BUCKETS = 32
T5_MAX_DIST = 128
T5_HEADS = DIFF_HEADS + NSA_HEADS
D_FF = -(-(8 * D_MODEL) // (3 * 256)) * 256
PLE_DIM = 256
N_EVEN = (DEPTH + 1) // 2
N_ODD = DEPTH // 2
EVEN_SPLITS = (RET_W, RET_W, RET_W, RET_W, FOX_W, FOX_W, FOX_W, FOX_HEADS)
ODD_SPLITS = (DIFF_W, DIFF_W, DIFF_W, NSA_W, NSA_KV_W, NSA_KV_W, NSA_KV_W, NSA_KV_W, NSA_KV_W, NSA_KV_W, 3 * NSA_HEADS)
EVEN_COLS = sum(EVEN_SPLITS)
ODD_COLS = sum(ODD_SPLITS)

kernel_name = 'hybrid_retention_fox_diff_nsa_trunk'


def rmsnorm(x, g):
    xf = x.astype(jnp.float32)
    y = xf * lax.rsqrt(jnp.mean(jnp.square(xf), -1, keepdims=True) + NORM_EPS) * g.astype(jnp.float32)
    return y.astype(x.dtype)


def split_cols(z, sizes):
    return jnp.split(z, [int(v) for v in np.cumsum(sizes)[:-1]], axis=-1)


def heads(a, n):
    B, S, _ = a.shape
    return a.reshape(B, S, n, -1).transpose(0, 2, 1, 3)


def merge(a):
    B, n, S, d = a.shape
    return a.transpose(0, 2, 1, 3).reshape(B, S, n * d)


def to_blocks(a, axis):
    shp = a.shape
    a = a.reshape(shp[:axis] + (shp[axis] // QBLOCK, QBLOCK) + shp[axis + 1:])
    return jnp.moveaxis(a, axis, 0)


def from_blocks(a, axis):
    a = jnp.moveaxis(a, 0, axis)
    shp = a.shape
    return a.reshape(shp[:axis] + (shp[axis] * shp[axis + 1],) + shp[axis + 2:])


def masked_softmax(logits, mask):
    logits = jnp.where(mask, logits.astype(jnp.float32), -jnp.inf)
    m = jnp.max(logits, -1, keepdims=True)
    m = jnp.where(jnp.isfinite(m), m, 0.0)
    e = jnp.where(mask, jnp.exp(logits - m), 0.0)
    return e / jnp.maximum(jnp.sum(e, -1, keepdims=True), 1e-30)


def t5_bucket(dist):
    n = jnp.maximum(dist, 0)
    max_exact = T5_BUCKETS // 2
    nf = jnp.maximum(n, 1).astype(jnp.float32)
    large = max_exact + (jnp.log(nf / max_exact) / math.log(T5_MAX_DIST / max_exact) * (T5_BUCKETS - max_exact)).astype(jnp.int32)
    large = jnp.minimum(large, T5_BUCKETS - 1)
    return jnp.where(n < max_exact, n, large)


def retention(q, k, v):
    B, H, S, d = q.shape
    C = RET_CHUNK
    N = S // C
    log_gamma = jnp.log1p(-jnp.exp2(-5.0 - jnp.arange(H, dtype=jnp.float32)))
    pos = jnp.arange(C, dtype=jnp.float32)
    rel = pos[:, None] - pos[None, :]
    intra_decay = jnp.where(rel >= 0, jnp.exp(log_gamma[:, None, None] * jnp.maximum(rel, 0.0)), 0.0)
    qc = q.reshape(B, H, N, C, d)
    kc = (k * d ** -0.5).reshape(B, H, N, C, d)
    vc = v.reshape(B, H, N, C, d)
    att = jnp.einsum('bhncd,bhnsd->bhncs', qc, kc) * intra_decay[None, :, None]
    y_intra = jnp.einsum('bhncs,bhnse->bhnce', att, vc)
    k_w = jnp.exp(log_gamma[:, None] * (C - 1 - pos))
    upd = jnp.einsum('bhnsd,bhnse->bhnde', kc * k_w[None, :, None, :, None], vc)
    chunk_decay = jnp.exp(log_gamma * C)[None, :, None, None]

    def step(state, u):
        return state * chunk_decay + u, state

    _, prev = lax.scan(step, jnp.zeros((B, H, d, d), jnp.float32), jnp.moveaxis(upd, 2, 0))
    prev = jnp.moveaxis(prev, 0, 2)
    q_w = jnp.exp(log_gamma[:, None] * (pos + 1.0))
    y_cross = jnp.einsum('bhncd,bhnde->bhnce', qc * q_w[None, :, None, :, None], prev)
    return (y_intra + y_cross).reshape(B, H, S, d)


def forgetting_attention(q, k, v, log_f):
    B, H, S, d = q.shape
    NB = S // QBLOCK
    c = jnp.cumsum(log_f, axis=-1)
    kpos = jnp.arange(S)
    scale = d ** -0.5

    def blk(args):
        qb, cb, i = args
        t = i * QBLOCK + jnp.arange(QBLOCK)
        logits = jnp.einsum('bhqd,bhsd->bhqs', qb, k).astype(jnp.float32) * scale + cb[..., None] - c[:, :, None, :]
        pr = masked_softmax(logits, kpos[None, :] <= t[:, None])
        return jnp.einsum('bhqs,bhsd->bhqd', pr.astype(v.dtype), v)

    out = lax.map(blk, (to_blocks(q, 2), to_blocks(c, 2), jnp.arange(NB)))
    return from_blocks(out, 2)


def diff_attention(q, k, v, lam, subln_g, lambda_init, t5_diff):
    B, H, S, _, e = q.shape
    NB = S // QBLOCK
    kpos = jnp.arange(S)
    lam_f = lam.astype(jnp.float32)
    lmbda = jnp.exp(jnp.sum(lam_f[0] * lam_f[1])) - jnp.exp(jnp.sum(lam_f[2] * lam_f[3])) + lambda_init
    table = t5_diff.astype(jnp.float32)
    scale = e ** -0.5

    def blk(args):
        qb, i = args
        t = i * QBLOCK + jnp.arange(QBLOCK)
        bias = jnp.transpose(table[t5_bucket(t[:, None] - kpos[None, :])], (2, 0, 1))
        logits = jnp.einsum('bhqme,bhsme->bhmqs', qb, k).astype(jnp.float32) * scale + bias[None, :, None]
        pr = masked_softmax(logits, kpos[None, :] <= t[:, None])
        w = pr[:, :, 0] - lmbda * pr[:, :, 1]
        return jnp.einsum('bhqs,bhsd->bhqd', w.astype(v.dtype), v)

    o = from_blocks(lax.map(blk, (to_blocks(q, 2), jnp.arange(NB))), 2)
    return (rmsnorm(o, subln_g).astype(jnp.float32) * (1.0 - lambda_init)).astype(v.dtype)


def compress_tokens(kv, pos_emb, w1, w2):
    B, G, S, d = kv.shape
    NC = (S - CMP_LEN) // CMP_STRIDE + 1
    gidx = jnp.arange(NC)[:, None] * CMP_STRIDE + jnp.arange(CMP_LEN)[None, :]
    blocks = kv[:, :, gidx] + pos_emb
    hid = jax.nn.silu(blocks.reshape(B, G, NC, CMP_LEN * d) @ w1)
    return hid @ w2


def nsa_attention(q, kc, vc, ks, vs, kw, vw, gates, t5_nsa):
    B, G, R, S, d = q.shape
    NB = S // QBLOCK
    NC = kc.shape[2]
    NS = S // SLC_LEN
    n_sel = min(SLC_TOPN, NS)
    scale = d ** -0.5
    tbl = t5_nsa.astype(jnp.float32).reshape(T5_BUCKETS, G, R)
    tbl_g = jnp.transpose(tbl, (1, 0, 2))
    cmp_end = jnp.arange(NC) * CMP_STRIDE + CMP_LEN - 1
    blk_start = jnp.arange(NS) * SLC_LEN
    overlap = ((cmp_end[:, None] - CMP_LEN + 1 <= blk_start[None, :] + SLC_LEN - 1) & (cmp_end[:, None] >= blk_start[None, :])).astype(jnp.float32)
    ks_b = ks.reshape(B, G, NS, SLC_LEN, d)
    vs_b = vs.reshape(B, G, NS, SLC_LEN, d)
    kw_pad = jnp.pad(kw, ((0, 0), (0, 0), (WINDOW, 0), (0, 0)))
    vw_pad = jnp.pad(vw, ((0, 0), (0, 0), (WINDOW, 0), (0, 0)))
    b_ix = jnp.arange(B)[:, None, None, None]
    g_ix = jnp.arange(G)[None, :, None, None]
    in_blk = jnp.arange(SLC_LEN)
    win_off = jnp.arange(QBLOCK + WINDOW) - WINDOW
    sel_ids = jnp.arange(NS)[None, :]

    def head_bias(bucket):
        return jnp.transpose(tbl[bucket], (2, 3, 0, 1))

    def blk(args):
        qb, gb, i = args
        q0 = i * QBLOCK
        t = q0 + jnp.arange(QBLOCK)
        lc = jnp.einsum('bgrqd,bgnd->bgrqn', qb, kc).astype(jnp.float32) * scale + head_bias(t5_bucket(t[:, None] - cmp_end[None, :]))
        p_cmp = masked_softmax(lc, cmp_end[None, :] <= t[:, None])
        o_cmp = jnp.einsum('bgrqn,bgnd->bgrqd', p_cmp.astype(vc.dtype), vc)
        imp = jnp.einsum('bgrqn,ns->bgqs', p_cmp, overlap)
        cur = (t // SLC_LEN)[:, None]
        forced = (sel_ids == 0) | (sel_ids == cur) | (sel_ids == cur - 1)
        valid = blk_start[None, :] <= t[:, None]
        score = jnp.where(forced, SEL_BIG, jnp.where(valid, imp, -SEL_BIG))
        _, idx = lax.top_k(score, n_sel)
        k_sel = ks_b[b_ix, g_ix, idx].reshape(B, G, QBLOCK, n_sel * SLC_LEN, d)
        v_sel = vs_b[b_ix, g_ix, idx].reshape(B, G, QBLOCK, n_sel * SLC_LEN, d)
        pos = (idx[..., None] * SLC_LEN + in_blk).reshape(B, G, QBLOCK, n_sel * SLC_LEN)
        dist = t[None, None, :, None] - pos
        bias_s = jnp.moveaxis(tbl_g[g_ix, t5_bucket(dist)], -1, 2)
        ls = jnp.einsum('bgrqd,bgqld->bgrql', qb, k_sel).astype(jnp.float32) * scale + bias_s
        p_slc = masked_softmax(ls, (dist >= 0)[:, :, None])
        o_slc = jnp.einsum('bgrql,bgqld->bgrqd', p_slc.astype(v_sel.dtype), v_sel)
        kwin = lax.dynamic_slice_in_dim(kw_pad, q0, QBLOCK + WINDOW, axis=2)
        vwin = lax.dynamic_slice_in_dim(vw_pad, q0, QBLOCK + WINDOW, axis=2)
        s = q0 + win_off
        dw = t[:, None] - s[None, :]
        mw = (dw >= 0) & (dw < WINDOW) & (s[None, :] >= 0)
        lw = jnp.einsum('bgrqd,bgkd->bgrqk', qb, kwin).astype(jnp.float32) * scale + head_bias(t5_bucket(dw))
        p_win = masked_softmax(lw, mw)
        o_win = jnp.einsum('bgrqk,bgkd->bgrqd', p_win.astype(vwin.dtype), vwin)
        g = jnp.transpose(gb.reshape(B, QBLOCK, 3, G, R), (2, 0, 3, 4, 1))[..., None]
        return g[0] * o_cmp + g[1] * o_slc + g[2] * o_win

    out = lax.map(blk, (to_blocks(q, 3), to_blocks(gates, 1), jnp.arange(NB)))
    return from_blocks(out, 3)


def even_mixer(h, w_in, ret_gn, fox_fb, w_out):
    z = h @ w_in
    rq, rk, rv, rg, fq, fk, fv, fl = split_cols(z, EVEN_SPLITS)
    f32 = jnp.float32
    y = retention(heads(rq, RET_HEADS).astype(f32), heads(rk, RET_HEADS).astype(f32), heads(rv, RET_HEADS).astype(f32))
    mu = jnp.mean(y, -1, keepdims=True)
    var = jnp.mean(jnp.square(y - mu), -1, keepdims=True)
    y = merge((y - mu) * lax.rsqrt(var + NORM_EPS)) * ret_gn.astype(f32)
    ret_out = (y * jax.nn.silu(rg.astype(f32))).astype(h.dtype)
    log_f = jax.nn.log_sigmoid(fl.astype(f32) + fox_fb.astype(f32)).transpose(0, 2, 1)
    fox_out = merge(forgetting_attention(heads(fq, FOX_HEADS), heads(fk, FOX_HEADS), heads(fv, FOX_HEADS), log_f))
    return jnp.concatenate([ret_out, fox_out.astype(h.dtype)], -1) @ w_out


def odd_mixer(h, w_in, diff_lambda, diff_subln, cmp_pos, cmp_w1, cmp_w2, w_out, t5_table, lambda_init):
    B, S, _ = h.shape
    z = h @ w_in
    dq, dk, dv, nq, ck, cv, sk, sv, wk, wv, gl = split_cols(z, ODD_SPLITS)
    dq = dq.reshape(B, S, DIFF_HEADS, 2, DIFF_SUB).transpose(0, 2, 1, 3, 4)
    dk = dk.reshape(B, S, DIFF_HEADS, 2, DIFF_SUB).transpose(0, 2, 1, 3, 4)
    diff_out = merge(diff_attention(dq, dk, heads(dv, DIFF_HEADS), diff_lambda, diff_subln, lambda_init, t5_table[:, :DIFF_HEADS]))
    nq = nq.reshape(B, S, NSA_KV_HEADS, NSA_GROUP, HEAD_DIM).transpose(0, 2, 3, 1, 4)
    kc = compress_tokens(heads(ck, NSA_KV_HEADS), cmp_pos[0], cmp_w1[0], cmp_w2[0])
    vc = compress_tokens(heads(cv, NSA_KV_HEADS), cmp_pos[1], cmp_w1[1], cmp_w2[1])
    gates = jax.nn.sigmoid(gl).reshape(B, S, 3, NSA_HEADS)
    o = nsa_attention(nq, kc, vc, heads(sk, NSA_KV_HEADS), heads(sv, NSA_KV_HEADS), heads(wk, NSA_KV_HEADS), heads(wv, NSA_KV_HEADS), gates, t5_table[:, DIFF_HEADS:])
    nsa_out = o.transpose(0, 3, 1, 2, 4).reshape(B, S, NSA_W)
    return jnp.concatenate([diff_out.astype(h.dtype), nsa_out.astype(h.dtype)], -1) @ w_out


def setup_inputs(seed: int = 0) -> dict:
    key = jax.random.key(seed)
    ks = jax.random.split(key, 24)
    f32 = jnp.float32

    def nrm(k, shape, scale):
        return jax.random.normal(k, shape, f32) * scale

    return {
        'x': nrm(ks[0], (BATCH, SEQ, D_MODEL), 1.0),
        'p': nrm(ks[1], (DEPTH, BATCH, SEQ, PLE_DIM), 1.0),
        'norm_mix': 1.0 + nrm(ks[2], (DEPTH, D_MODEL), 0.02),
        'norm_ffn': 1.0 + nrm(ks[3], (DEPTH, D_MODEL), 0.02),
        'w_in_even': nrm(ks[4], (N_EVEN, D_MODEL, EVEN_COLS), D_MODEL ** -0.5),
        'ret_gn': 1.0 + nrm(ks[5], (N_EVEN, RET_W), 0.02),
        'fox_fb': FOX_GATE_BIAS + nrm(ks[6], (N_EVEN, FOX_HEADS), 0.5),
        'w_out_even': nrm(ks[7], (N_EVEN, MIX_W, D_MODEL), MIX_W ** -0.5),
        'w_in_odd': nrm(ks[8], (N_ODD, D_MODEL, ODD_COLS), D_MODEL ** -0.5),
        'diff_lambda': nrm(ks[9], (N_ODD, 4, DIFF_SUB), 0.1),
        'diff_subln': 1.0 + nrm(ks[10], (N_ODD, HEAD_DIM), 0.02),
        'cmp_pos': nrm(ks[11], (N_ODD, 2, CMP_LEN, HEAD_DIM), 0.02),
        'cmp_w1': nrm(ks[12], (N_ODD, 2, CMP_LEN * HEAD_DIM, HEAD_DIM), (CMP_LEN * HEAD_DIM) ** -0.5),
        'cmp_w2': nrm(ks[13], (N_ODD, 2, HEAD_DIM, HEAD_DIM), HEAD_DIM ** -0.5),
        'w_out_odd': nrm(ks[14], (N_ODD, MIX_W, D_MODEL), MIX_W ** -0.5),
        't5_table': nrm(ks[15], (T5_BUCKETS, T5_HEADS), 0.2),
        'ffn_gate': nrm(ks[16], (DEPTH, D_MODEL, D_FF), D_MODEL ** -0.5),
        'ffn_up': nrm(ks[17], (DEPTH, D_MODEL, D_FF), D_MODEL ** -0.5),
        'ffn_down': nrm(ks[18], (DEPTH, D_FF, D_MODEL), D_FF ** -0.5),
        'ple_gate': nrm(ks[19], (DEPTH, D_MODEL, D_MODEL), D_MODEL ** -0.5),
        'ple_proj': nrm(ks[20], (DEPTH, PLE_DIM, D_MODEL), PLE_DIM ** -0.5),
        'final_norm': 1.0 + nrm(ks[21], (D_MODEL,), 0.02),
    }


def reference(x, p, norm_mix, norm_ffn, w_in_even, ret_gn, fox_fb, w_out_even, w_in_odd, diff_lambda, diff_subln, cmp_pos, cmp_w1, cmp_w2, w_out_odd, t5_table, ffn_gate, ffn_up, ffn_down, ple_gate, ple_proj, final_norm):
    for i in range(DEPTH):
        j = i // 2
        h = rmsnorm(x, norm_mix[i])
        if i % 2 == 0:
            mix = even_mixer(h, w_in_even[j], ret_gn[j], fox_fb[j], w_out_even[j])
        else:
            lambda_init = 0.8 - 0.6 * math.exp(-0.3 * i)
            mix = odd_mixer(h, w_in_odd[j], diff_lambda[j], diff_subln[j], cmp_pos[j], cmp_w1[j], cmp_w2[j], w_out_odd[j], t5_table, lambda_init)
        x = x + mix
        h = rmsnorm(x, norm_ffn[i])
        x = x + (jax.nn.silu(h @ ffn_gate[i]) * (h @ ffn_up[i])) @ ffn_down[i]
        x = x + jax.nn.sigmoid(x @ ple_gate[i]) * (p[i] @ ple_proj[i])
    return rmsnorm(x, final_norm)
```

```python
import math
import numpy as np
import ml_dtypes
from contextlib import ExitStack
import concourse.bass as bass
import concourse.mybir as mybir
from concourse.bass_utils import run_bass_kernel_spmd

F32 = mybir.dt.float32
BF16 = mybir.dt.bfloat16
AF = mybir.ActivationFunctionType
ALU = mybir.AluOpType
AX = mybir.AxisListType

D = 2048
KC = D // 128
DFF = 5632
FC = DFF // 128
NQ = 4
FQ = FC // NQ
PLE = 256
EPS = 1e-6
NCORES = 8


class Dep:
    __slots__ = ("w", "r")

    def __init__(self):
        self.w = None
        self.r = {}


class Eng:
    def __init__(self, key, eng, sem):
        self.key = key
        self.eng = eng
        self.sem = sem
        self.count = 0
        self.seen = {}


class KB:
    def __init__(self, nc, stack, n_dma_sems=48):
        self.nc = nc
        self.stack = stack
        self.sems = {}
        self.E = {}
        for key, eng in (("pe", nc.tensor), ("act", nc.scalar), ("dve", nc.vector),
                         ("pool", nc.gpsimd), ("sp", nc.sync)):
            sem = stack.enter_context(nc.semaphore("s_" + key))
            self.sems[key] = sem
            self.E[key] = Eng(key, eng, sem)
        self.dma_sems = []
        for i in range(n_dma_sems):
            k = "d%d" % i
            self.sems[k] = stack.enter_context(nc.semaphore("s_" + k))
            self.dma_sems.append([k, 0])
        self.dma_rr = 0
        self.n_inst = 0
        self._uid = 0

    def sb(self, shape, dtype, name=None):
        self._uid += 1
        return self.stack.enter_context(self.nc.sbuf_tensor("sb%d_%s" % (self._uid, name or ""), list(shape), dtype))

    def ps(self, shape, dtype, name=None):
        self._uid += 1
        return self.stack.enter_context(self.nc.psum_tensor("ps%d_%s" % (self._uid, name or ""), list(shape), dtype))

    def _wait(self, E, semkey, val):
        if val is None or val <= 0:
            return
        if E.seen.get(semkey, 0) >= val:
            return
        if semkey == E.key and E.key == "pe":
            return
        E.eng.wait_ge(self.sems[semkey], val)
        E.seen[semkey] = val

    def _deps(self, E, reads, writes):
        for d in reads:
            if d.w is not None:
                self._wait(E, d.w[0], d.w[1])
        for d in writes:
            if d.w is not None:
                self._wait(E, d.w[0], d.w[1])
            for sk, v in d.r.items():
                self._wait(E, sk, v)

    def _mark(self, ev, reads, writes):
        for d in reads:
            if d.r.get(ev[0], 0) < ev[1]:
                d.r[ev[0]] = ev[1]
        for d in writes:
            d.w = ev
            d.r = {}

    def op(self, ek, fn, reads=(), writes=(), inc=True):
        E = self.E[ek]
        self._deps(E, reads, writes)
        ins = fn(E.eng)
        inc = True
        if inc:
            E.count += 1
            ins.then_inc(E.sem, 1)
            ev = (E.key, E.count)
        else:
            ev = (E.key, E.count + 1)
        self._mark(ev, reads, writes)
        self.n_inst += 1
        return ev

    def dma(self, ek, out, in_, reads=(), writes=(), sw=None, **kw):
        E = self.E[ek]
        self._deps(E, reads, writes)
        if sw is not None:
            assert ek == "pool"
            if sw.gen > 0:
                self._wait(E, ("sw", sw.id, sw.gen), 16)
                E.eng.sem_clear(sw.sem)
            sw.gen += 1
            key = ("sw", sw.id, sw.gen)
            self.sems[key] = sw.sem
            ins = E.eng.dma_start(out=out, in_=in_, **kw)
            ins.then_inc(sw.sem, 16)
            ev = (key, 16)
            self._mark(ev, reads, writes)
            self.n_inst += 1
            return ev
        slot = self.dma_sems[self.dma_rr]
        self.dma_rr = (self.dma_rr + 1) % len(self.dma_sems)
        self._wait(E, slot[0], slot[1])
        ins = E.eng.dma_start(out=out, in_=in_, **kw)
        slot[1] += 16
        ins.then_inc(self.sems[slot[0]], 16)
        ev = (slot[0], slot[1])
        self._mark(ev, reads, writes)
        self.n_inst += 1
        return ev

    def barrier(self):
        for E in self.E.values():
            for E2 in self.E.values():
                if E2.count > 0:
                    if E2.key == E.key and E.key == "pe":
                        continue
                    self._wait(E, E2.key, E2.count)
            for k, v in self.dma_sems:
                self._wait(E, k, v)

    def finish(self, out_deps):
        E = self.E["sp"]
        for d in out_deps:
            if d.w is not None:
                self._wait(E, d.w[0], d.w[1])


class SwSlot:
    _n = 0

    def __init__(self, kb):
        SwSlot._n += 1
        self.id = SwSlot._n
        self.sem = kb.stack.enter_context(kb.nc.semaphore("s_sw%d" % self.id))
        self.gen = 0


class PsumPool:
    def __init__(self, kb, n=8):
        self.tiles = [kb.ps([128, 512], F32, name="psb%d" % i) for i in range(n)]
        self.deps = [Dep() for _ in range(n)]
        self.i = 0

    def get(self):
        t, d = self.tiles[self.i], self.deps[self.i]
        self.i = (self.i + 1) % len(self.tiles)
        return t, d

    def sub(self, idx):
        p = PsumPool.__new__(PsumPool)
        p.tiles = [self.tiles[i] for i in idx]
        p.deps = [self.deps[i] for i in idx]
        p.i = 0
        return p


class Ring:
    def __init__(self, kb, n, shape, dtype, name, sw=False):
        self.tiles = [kb.sb(shape, dtype, name="%s%d" % (name, i)) for i in range(n)]
        self.deps = [Dep() for _ in range(n)]
        self.sw = [SwSlot(kb) for _ in range(n)] if sw else None
        self.i = 0

    def get(self):
        t, d = self.tiles[self.i], self.deps[self.i]
        self.i = (self.i + 1) % len(self.tiles)
        return t, d

    def get3(self):
        i = self.i
        self.i = (self.i + 1) % len(self.tiles)
        return self.tiles[i], self.deps[i], self.sw[i]


def rstd_from_ss(kb, ss_ps, dss, out_sb, dout, n, inv_d=1.0 / D):
    kb.op("dve", lambda e: e.tensor_scalar(out_sb, ss_ps, inv_d, EPS, op0=ALU.mult, op1=ALU.add),
          reads=[dss], writes=[dout])
    kb.op("dve", lambda e: e.reciprocal(out_sb, out_sb), reads=[dout], writes=[dout])
    kb.op("act", lambda e: e.activation(out_sb, out_sb, AF.Sqrt), reads=[dout], writes=[dout])


class WLoader:
    def __init__(self, kb, nstage=2, stage_elems=2048):
        self.kb = kb
        self.stage = Ring(kb, nstage, [128, stage_elems], F32, "wst")
        self.k = 0

    def load(self, dst_ap2d, ddst, src_ap2d, n, q="sp"):
        kb = self.kb
        stg, dstg = self.stage.get()
        kb.dma(q, stg[:, 0:n], src_ap2d, writes=[dstg])
        eng = ("pool", "dve")[self.k % 2]
        self.k += 1
        kb.op(eng, lambda e: e.tensor_copy(dst_ap2d, stg[:, 0:n]), reads=[dstg], writes=[ddst])


def build_phase_b(T, final):
    TG = 1024
    NG = T // TG
    NH = TG // 512
    nc = bass.Bass("TRN2", target_bir_lowering=False)
    xT_in = nc.dram_tensor("xT", [D, T], F32, kind="ExternalInput").ap()
    mixT = nc.dram_tensor("mixT", [D, T], BF16, kind="ExternalInput").ap()
    pT = nc.dram_tensor("pT", [PLE, T], F32, kind="ExternalInput").ap()
    w_out = nc.dram_tensor("w_out", [KC, 128, KC * 128], F32, kind="ExternalInput").ap()
    w_gate = nc.dram_tensor("w_gate", [FC, 128, KC * 128], F32, kind="ExternalInput").ap()
    w_up = nc.dram_tensor("w_up", [FC, 128, KC * 128], F32, kind="ExternalInput").ap()
    w_down = nc.dram_tensor("w_down", [NQ, KC, 128, FQ * 128], F32, kind="ExternalInput").ap()
    w_pg = nc.dram_tensor("w_pg", [KC, 128, KC * 128], F32, kind="ExternalInput").ap()
    w_pp = nc.dram_tensor("w_pp", [KC, 128, 2 * 128], F32, kind="ExternalInput").ap()
    g_ffn = nc.dram_tensor("g_ffn", [128, KC], F32, kind="ExternalInput").ap()
    g_fin = nc.dram_tensor("g_fin", [128, KC], F32, kind="ExternalInput").ap()
    outT = nc.dram_tensor("outT", [D, T], F32, kind="ExternalOutput").ap()

    with ExitStack() as st:
        kb = KB(nc, st)
        pp = PsumPool(kb)
        xres = kb.sb([128, KC, TG], F32, name="xres")
        dx = [Dep() for _ in range(KC)]
        actT = kb.sb([128, KC, TG], BF16, name="actT")
        da = [Dep() for _ in range(KC)]
        AT = kb.sb([128, FQ, TG], BF16, name="AT")
        dA = [Dep() for _ in range(FQ)]
        pTb = kb.sb([128, 2, TG], BF16, name="pTb")
        dp = Dep()
        gf = kb.sb([128, KC], F32, name="gf")
        dgf = Dep()
        gl = kb.sb([128, KC], F32, name="gl")
        dgl = Dep()
        ones = kb.sb([128, 128], BF16, name="ones")
        dones = Dep()
        rstd = kb.sb([128, TG], F32, name="rstd")
        drs = Dep()
        wl = WLoader(kb, nstage=3)
        wring = Ring(kb, 4, [128, KC * 128], BF16, "wr")
        wdring = Ring(kb, 2, [128, FQ * 128], BF16, "wd")
        wpring = Ring(kb, 2, [128, 2 * 128], BF16, "wp")
        sqring = Ring(kb, 2, [128, TG], BF16, "sq")
        tring = Ring(kb, 3, [128, 512], BF16, "tb")
        fring = Ring(kb, 2, [128, 512], F32, "tf")
        outdeps = []

        kb.dma("sp", gf[:], g_ffn, writes=[dgf])
        kb.dma("sp", gl[:], g_fin, writes=[dgl])
        kb.op("pool", lambda e: e.memset(ones[:], 1.0), writes=[dones])

        def rms_stats(src_deps):
            pss = [pp.get() for _ in range(NH)]
            for kc in range(KC):
                sq, dsq = sqring.get()
                kb.op("act", lambda e: e.activation(sq[:], xres[:, kc, :], AF.Square),
                      reads=[src_deps[kc]], writes=[dsq])
                for h, (pt, dpt) in enumerate(pss):
                    kb.op("pe", lambda e: e.matmul(pt[:], ones[:], sq[:, h * 512:(h + 1) * 512],
                                                   start=(kc == 0), stop=(kc == KC - 1)),
                          reads=[dsq, dones], writes=[dpt], inc=(kc == KC - 1))
            for h, (pt, dpt) in enumerate(pss):
                rstd_from_ss(kb, pt[:], dpt, rstd[:, h * 512:(h + 1) * 512], drs, 512)

        for g in range(NG):
            t0 = g * TG
            for kc in range(KC):
                kb.dma("sp", xres[:, kc, :], xT_in[kc * 128:(kc + 1) * 128, t0:t0 + TG], writes=[dx[kc]])
                kb.dma("act", actT[:, kc, :], mixT[kc * 128:(kc + 1) * 128, t0:t0 + TG], writes=[da[kc]])
            for j in range(2):
                wl.load(pTb[:, j, 0:512], dp, pT[j * 128:(j + 1) * 128, t0:t0 + 512], 512)
                wl.load(pTb[:, j, 512:1024], dp, pT[j * 128:(j + 1) * 128, t0 + 512:t0 + 1024], 512)
            for cb in range(KC):
                w, dw = wring.get()
                wl.load(w[:], dw, w_out[cb], KC * 128)
                for h in range(NH):
                    sl = slice(h * 512, (h + 1) * 512)
                    pt, dpt = pp.get()
                    for kc in range(KC):
                        kb.op("pe", lambda e: e.matmul(pt[:], w[:, kc * 128:(kc + 1) * 128], actT[:, kc, sl],
                                                       start=(kc == 0), stop=(kc == KC - 1)),
                              reads=[dw, da[kc]], writes=[dpt])
                    kb.op("dve", lambda e: e.tensor_tensor(xres[:, cb, sl], xres[:, cb, sl], pt[:], op=ALU.add),
                          reads=[dpt, dx[cb]], writes=[dx[cb]])
            rms_stats(dx)
            for kc in range(KC):
                kb.op("dve", lambda e: e.scalar_tensor_tensor(actT[:, kc, :], xres[:, kc, :], gf[:, kc:kc + 1],
                                                              rstd[:], op0=ALU.mult, op1=ALU.mult),
                      reads=[dx[kc], dgf, drs], writes=[da[kc]])
            for fq in range(NQ):
                for fc in range(FQ):
                    f = fq * FQ + fc
                    wg, dwg = wring.get()
                    wl.load(wg[:], dwg, w_gate[f], KC * 128)
                    wu, dwu = wring.get()
                    wl.load(wu[:], dwu, w_up[f], KC * 128)
                    for h in range(NH):
                        sl = slice(h * 512, (h + 1) * 512)
                        pg, dpg = pp.get()
                        pu, dpu = pp.get()
                        for kc in range(KC):
                            kb.op("pe", lambda e: e.matmul(pg[:], wg[:, kc * 128:(kc + 1) * 128], actT[:, kc, sl],
                                                           start=(kc == 0), stop=(kc == KC - 1)),
                                  reads=[dwg, da[kc]], writes=[dpg])
                        for kc in range(KC):
                            kb.op("pe", lambda e: e.matmul(pu[:], wu[:, kc * 128:(kc + 1) * 128], actT[:, kc, sl],
                                                           start=(kc == 0), stop=(kc == KC - 1)),
                                  reads=[dwu, da[kc]], writes=[dpu])
                        tb, dtb = tring.get()
                        kb.op("act", lambda e: e.activation(tb[:], pg[:], AF.Silu), reads=[dpg], writes=[dtb])
                        kb.op("dve", lambda e: e.tensor_tensor(AT[:, fc, sl], tb[:], pu[:], op=ALU.mult),
                              reads=[dtb, dpu], writes=[dA[fc]])
                for cb in range(KC):
                    wd, dwd = wdring.get()
                    wl.load(wd[:], dwd, w_down[fq, cb], FQ * 128)
                    for h in range(NH):
                        sl = slice(h * 512, (h + 1) * 512)
                        pt, dpt = pp.get()
                        for fc in range(FQ):
                            kb.op("pe", lambda e: e.matmul(pt[:], wd[:, fc * 128:(fc + 1) * 128], AT[:, fc, sl],
                                                           start=(fc == 0), stop=(fc == FQ - 1)),
                                  reads=[dwd, dA[fc]], writes=[dpt])
                        kb.op("dve", lambda e: e.tensor_tensor(xres[:, cb, sl], xres[:, cb, sl], pt[:], op=ALU.add),
                              reads=[dpt, dx[cb]], writes=[dx[cb]])
            for kc in range(KC):
                kb.op("act", lambda e: e.activation(actT[:, kc, :], xres[:, kc, :], AF.Copy),
                      reads=[dx[kc]], writes=[da[kc]])
            for cb in range(KC):
                w, dw = wring.get()
                wl.load(w[:], dw, w_pg[cb], KC * 128)
                wq, dwq = wpring.get()
                wl.load(wq[:], dwq, w_pp[cb], 256)
                for h in range(NH):
                    sl = slice(h * 512, (h + 1) * 512)
                    pg, dpg = pp.get()
                    pq, dpq = pp.get()
                    for kc in range(KC):
                        kb.op("pe", lambda e: e.matmul(pg[:], w[:, kc * 128:(kc + 1) * 128], actT[:, kc, sl],
                                                       start=(kc == 0), stop=(kc == KC - 1)),
                              reads=[dw, da[kc]], writes=[dpg])
                    for j in range(2):
                        kb.op("pe", lambda e: e.matmul(pq[:], wq[:, j * 128:(j + 1) * 128], pTb[:, j, sl],
                                                       start=(j == 0), stop=(j == 1)),
                              reads=[dwq, dp], writes=[dpq], inc=(j == 1))
                    tf, dtf = fring.get()
                    kb.op("act", lambda e: e.activation(tf[:], pg[:], AF.Sigmoid), reads=[dpg], writes=[dtf])
                    kb.op("dve", lambda e: e.tensor_tensor(tf[:], tf[:], pq[:], op=ALU.mult),
                          reads=[dtf, dpq], writes=[dtf])
                    kb.op("dve", lambda e: e.tensor_tensor(xres[:, cb, sl], xres[:, cb, sl], tf[:], op=ALU.add),
                          reads=[dtf, dx[cb]], writes=[dx[cb]])
            if final:
                rms_stats(dx)
                for kc in range(KC):
                    kb.op("dve", lambda e: e.scalar_tensor_tensor(xres[:, kc, :], xres[:, kc, :], gl[:, kc:kc + 1],
                                                                  rstd[:], op0=ALU.mult, op1=ALU.mult),
                          reads=[dx[kc], dgl, drs], writes=[dx[kc]])
            for kc in range(KC):
                do = Dep()
                kb.dma("sp", outT[kc * 128:(kc + 1) * 128, t0:t0 + TG], xres[:, kc, :], reads=[dx[kc]], writes=[do])
                outdeps.append(do)
        kb.finish(outdeps)
    return nc


def prep_phase_b_weights(w_out, ffn_gate, ffn_up, ffn_down, ple_gate, ple_proj, norm_ffn, final_norm):
    def blk(w, kcn):
        K, C = w.shape
        return np.ascontiguousarray(w.reshape(kcn, 128, C // 128, 128).transpose(2, 1, 0, 3)).reshape(C // 128, 128, kcn * 128)
    d = {}
    d["w_out"] = blk(w_out, KC)
    d["w_gate"] = blk(ffn_gate, KC)
    d["w_up"] = blk(ffn_up, KC)
    d["w_down"] = np.ascontiguousarray(
        ffn_down.reshape(NQ, FQ, 128, KC, 128).transpose(0, 3, 2, 1, 4)).reshape(NQ, KC, 128, FQ * 128)
    d["w_pg"] = blk(ple_gate, KC)
    d["w_pp"] = blk(ple_proj, 2)
    d["g_ffn"] = np.ascontiguousarray(norm_ffn.reshape(KC, 128).T)
    d["g_fin"] = np.ascontiguousarray(final_norm.reshape(KC, 128).T)
    return d


def load_weights_scaled(kb, wl, dst, ddst, src, gcol, dg, ncols):
    per = max(1, 2048 // ncols)
    kc = 0
    while kc < KC:
        n = min(per, KC - kc)
        stg, dstg = wl.stage.get()
        kb.dma("sp", stg[:, 0:n * ncols].rearrange("p (a b) -> p a b", a=n), src[:, kc:kc + n, :], writes=[dstg])
        for a in range(n):
            kb.op("dve", lambda e: e.tensor_scalar(dst[:, kc + a, :], stg[:, a * ncols:(a + 1) * ncols],
                                                   gcol[:, kc + a:kc + a + 1], None, op0=ALU.mult),
                  reads=[dstg, dg], writes=[ddst])
        kc += n


def projection_pass(kb, pp, S, xT, wf, dwf, NF, wt, dwt, NTM, f_out, t_out, ones, dones, fl_cb=None,
                    xring=None, hring=None, sqring=None):
    NG = S // 512
    fdeps = [[Dep() for _ in range(NG)] for _ in range(NF)]
    rsb = kb.sb([128, 512], F32, name="rs_bc")
    drsb = Dep()
    rsc = kb.sb([128, 4], F32, name="rs_col")
    drsc = Dep()
    evr = Ring(kb, 2 * (NF + 4) + 2, [128, 512], BF16, "pev")
    pend = []
    e0 = kb.sb([128, 1], F32, name="e0")
    de0 = Dep()
    kb.op("pool", lambda e: e.memset(e0[:], 0.0), writes=[de0])
    kb.op("pool", lambda e: e.memset(e0[0:1, :], 1.0), writes=[de0])
    tmpc = Ring(kb, 2, [128, 1], F32, "ptc")
    for g in range(NG):
        t0 = g * 512
        hT, dh = hring.get()
        pss, dpss = pp.get()
        psc, dpsc = pp.get()
        for kc in range(KC):
            xt, dxt = xring.get()
            kb.dma("sp", xt[:], xT[kc * 128:(kc + 1) * 128, t0:t0 + 512], writes=[dxt])
            kb.op("dve", lambda e: e.tensor_copy(hT[:, kc, :], xt[:]), reads=[dxt], writes=[dh])
            sq, dsq = sqring.get()
            kb.op("act", lambda e: e.activation(sq[:], xt[:], AF.Square), reads=[dxt], writes=[dsq])
            kb.op("pe", lambda e: e.matmul(pss[:], ones[:], sq[:], start=(kc == 0), stop=(kc == KC - 1)),
                  reads=[dsq, dones], writes=[dpss])
        for (o_, i_, r_, w_) in pend:
            kb.dma("act", o_, i_, reads=r_, writes=w_)
        pend.clear()
        rstd_from_ss(kb, pss[:], dpss, rsb[:], drsb, 512)
        for gi in range(NF):
            pt, dpt = pp.get()
            for kc in range(KC):
                kb.op("pe", lambda e: e.matmul(pt[:], wf[:, gi, kc, :], hT[:, kc, :],
                                               start=(kc == 0), stop=(kc == KC - 1)),
                      reads=[dwf, dh], writes=[dpt], inc=(kc == KC - 1))
            ev, dev = evr.get()
            kb.op("dve", lambda e: e.tensor_tensor(ev[:], pt[:], rsb[:], op=ALU.mult),
                  reads=[dpt, drsb], writes=[dev])
            pend.append((f_out[gi][:, t0:t0 + 512], ev[:], [dev], [fdeps[gi][g]]))
        for tt in range(4):
            kb.op("pe", lambda e: e.matmul(psc[:, tt:tt + 1], rsb[:, tt * 128:(tt + 1) * 128], e0[:, 0:1],
                                           start=True, stop=True),
                  reads=[drsb, de0], writes=[dpsc])
        kb.op("dve", lambda e: e.tensor_copy(rsc[:], psc[:, 0:4]), reads=[dpsc], writes=[drsc])
        for tt in range(4):
            tok0 = t0 + tt * 128
            if NTM > 0:
                pt, dpt = pp.get()
                for kc in range(KC):
                    kb.op("pe", lambda e: e.matmul(pt[:, 0:NTM], hT[:, kc, tt * 128:(tt + 1) * 128], wt[:, kc, 0:NTM],
                                                   start=(kc == 0), stop=(kc == KC - 1)),
                          reads=[dwt, dh], writes=[dpt], inc=(kc == KC - 1))
                ev, dev = evr.get()
                for (c0, n, dram, deps, sc, dsc) in t_out:
                    if sc is None:
                        kb.op("act", lambda e: e.activation(ev[:, c0:c0 + n], pt[:, c0:c0 + n], AF.Copy,
                                                            scale=rsc[:, tt:tt + 1]),
                              reads=[dpt, drsc], writes=[dev])
                    else:
                        tc_, dtc = tmpc.get()
                        kb.op("dve", lambda e: e.tensor_tensor(tc_[:], rsc[:, tt:tt + 1], sc, op=ALU.mult),
                              reads=[drsc, dsc], writes=[dtc])
                        kb.op("act", lambda e: e.activation(ev[:, c0:c0 + n], pt[:, c0:c0 + n], AF.Copy,
                                                            scale=tc_[:]),
                              reads=[dpt, dtc], writes=[dev])
                for (c0, n, dram, deps, sc, dsc) in t_out:
                    pend.append((dram[tok0:tok0 + 128, :], ev[:, c0:c0 + n], [dev], [deps[g * 4 + tt]]))
            if fl_cb is not None:
                pf, dpf = pp.get()
                for kc in range(KC):
                    kb.op("pe", lambda e: e.matmul(pf[:, 0:1], hT[:, kc, tt * 128:(tt + 1) * 128], wt[:, kc, NTM:NTM + 1],
                                                   start=(kc == 0), stop=(kc == KC - 1)),
                          reads=[dwt, dh], writes=[dpf], inc=(kc == KC - 1))
                fl_cb(g * 4 + tt, pf, dpf, rsc[:, tt:tt + 1], drsc)
    for (o_, i_, r_, w_) in pend:
        kb.dma("act", o_, i_, reads=r_, writes=w_)
    pend.clear()
    return fdeps


def attention_pass(kb, pp, S, qT_dram, dq_deps, krow0, krows, kT, dkT, v, dv, scale, bias_fn, mult_fn, ktiles_fn,
                   out_fn, ones, dones, extra_fn=None, pre_fn=None, name="at", look=1):
    NQT = S // 512
    GSZ = 2
    pacc = pp.sub([0, 1])
    pl = pp.sub([2, 3])
    pst = pp.sub([4, 5, 6, 7])
    qring = Ring(kb, 3, [128, 512], BF16, name + "q")
    if krows < 128:
        for t_, d_ in zip(qring.tiles, qring.deps):
            kb.op("pool", lambda e: e.memset(t_[:], 0.0), writes=[d_])
    pring = Ring(kb, 8, [128, 512], BF16, name + "p")
    mring = Ring(kb, 8, [128, 512], BF16, name + "m")
    lring = Ring(kb, 3, [128, 512], F32, name + "l")
    onesf = kb.sb([128, 128], F32, name=name + "1f")
    donf = Dep()
    kb.op("pool", lambda e: e.memset(onesf[:], 1.0), writes=[donf])
    items = []
    for j in range(NQT):
        kts = ktiles_fn(j)
        for n_i, i in enumerate(kts):
            items.append((j, n_i, i, len(kts)))
    qstate = {}
    sstate = {}

    def issue_s(t, inc=True):
        j, n_i, i, nk = items[t]
        if n_i == 0:
            qt, dqt = qring.get()
            kb.dma("sp", qt[krow0:krow0 + krows, :], qT_dram[krow0:krow0 + krows, j * 512:(j + 1) * 512],
                   reads=[dq_deps[j]], writes=[dqt])
            bias_ap, dbias = bias_fn(j)
            if pre_fn is not None:
                pre_fn(j)
            qstate[j] = dict(qt=qt, dqt=dqt, bias=bias_ap, dbias=dbias)
        q = qstate[j]
        st, dst = pst.get()
        ex = extra_fn(i, j) if extra_fn is not None else None
        kb.op("pe", lambda e: e.matmul(st[:], kT[:, i * 128:(i + 1) * 128],
                                       q["qt"][:, :], start=True, stop=(ex is None)),
              reads=[dkT, q["dqt"]], writes=[dst], inc=(inc and ex is None))
        if ex is not None:
            lhsT_ap, rhs_ap, exdeps = ex
            kb.op("pe", lambda e: e.matmul(st[:], lhsT_ap, rhs_ap, start=False, stop=True),
                  reads=exdeps, writes=[dst], inc=inc)
        pt_, dpt_ = pring.get()
        b = q["bias"](i) if callable(q["bias"]) else q["bias"]
        kb.op("act", lambda e: e.activation(pt_[:], st[:], AF.Exp, bias=b, scale=scale),
              reads=[dst, q["dbias"]], writes=[dpt_])
        m = mult_fn(i, j)
        if m is not None:
            m_ap, dm = m
            pm, dpm = mring.get()
            kb.op("dve", lambda e: e.tensor_tensor(pm[:], pt_[:], m_ap, op=ALU.mult),
                  reads=[dpt_, dm], writes=[dpm])
            pt_, dpt_ = pm, dpm
        pe_l = (n_i % 2 == 1)
        if n_i == 0:
            la, dla = lring.get()
            q["la"], q["dla"] = la, dla
            kb.op("dve", lambda e: e.tensor_copy(la[:], pt_[:]), reads=[dpt_], writes=[dla])
        elif not pe_l:
            la, dla = q["la"], q["dla"]
            kb.op("dve", lambda e: e.tensor_tensor(la[:], la[:], pt_[:], op=ALU.add), reads=[dpt_, dla], writes=[dla])
        sstate[t] = (pt_, dpt_, pe_l)

    def issue_pv(t, inc=True):
        j, n_i, i, nk = items[t]
        q = qstate[j]
        pt_, dpt_, pe_l = sstate.pop(t)
        if n_i == 0:
            q["O"], q["dO"] = pacc.get()
            q["L"], q["dL"] = pl.get()
            q["lstart"] = True
        first, last = (n_i == 0), (n_i == nk - 1)
        kb.op("pe", lambda e: e.matmul(q["O"][:], v[:, i, :], pt_[:], start=first, stop=last),
              reads=[dv, dpt_], writes=[q["dO"]])
        if pe_l:
            ls = q["lstart"]
            kb.op("pe", lambda e: e.matmul(q["L"][:], ones[:], pt_[:], start=ls, stop=False),
                  reads=[dones, dpt_], writes=[q["dL"]])
            q["lstart"] = False
        if last:
            ls = q["lstart"]
            kb.op("pe", lambda e: e.matmul(q["L"][:], onesf[:], q["la"][:], start=ls, stop=True),
                  reads=[donf, q["dla"]], writes=[q["dL"]])
            out_fn(j, q["O"], q["dO"], q["L"], q["dL"])
            del qstate[j]

    n = len(items)
    ngr = (n + GSZ - 1) // GSZ
    for g in range(ngr + look):
        if g < ngr:
            hi = min(n, (g + 1) * GSZ)
            for t in range(g * GSZ, hi):
                issue_s(t, inc=(t == hi - 1))
        gp = g - look
        if gp >= 0:
            ts = list(range(gp * GSZ, min(n, (gp + 1) * GSZ)))
            for t in reversed(ts):
                d = sstate[t][1]
                if d.w is not None:
                    kb._wait(kb.E["pe"], d.w[0], d.w[1])
            for t in ts:
                issue_pv(t, inc=(t == ts[-1]))


def build_phase_a0(S):
    NT = S // 128
    NG = S // 512
    nc = bass.Bass("TRN2", target_bir_lowering=False)
    xT = nc.dram_tensor("xT", [D, S], F32, kind="ExternalInput").ap()
    wf_d = nc.dram_tensor("wf", [128, 5, KC, 128], F32, kind="ExternalInput").ap()
    wt_d = nc.dram_tensor("wt", [128, KC, 385], F32, kind="ExternalInput").ap()
    gmix_d = nc.dram_tensor("gmix", [128, KC], F32, kind="ExternalInput").ap()
    fb_d = nc.dram_tensor("fb", [128, 1], F32, kind="ExternalInput").ap()
    gn_d = nc.dram_tensor("gn", [128, 1], F32, kind="ExternalInput").ap()
    kw_d = nc.dram_tensor("kw", [128, 1], F32, kind="ExternalInput").ap()
    cd_d = nc.dram_tensor("cd", [128, 1], F32, kind="ExternalInput").ap()
    qw_d = nc.dram_tensor("qw", [128, 512], F32, kind="ExternalInput").ap()
    dmat_d = nc.dram_tensor("dmat", [128, 512], F32, kind="ExternalInput").ap()
    mu_d = nc.dram_tensor("mu", [128, 4, 512], BF16, kind="ExternalInput").ap()
    tri_d = nc.dram_tensor("tri", [128, 3, 128], F32, kind="ExternalInput").ap()
    mixT = nc.dram_tensor("mixT", [256, S], BF16, kind="ExternalOutput").ap()
    names = ["rqT", "rkT", "rgT", "fqT", "fkT"]
    fT = [nc.dram_tensor("s_" + n, [128, S], BF16, kind="Internal").ap() for n in names]
    rkw_d = nc.dram_tensor("s_rkw", [S, 128], BF16, kind="Internal").ap()
    rv_d = nc.dram_tensor("s_rv", [S, 128], BF16, kind="Internal").ap()
    fv_d = nc.dram_tensor("s_fv", [S, 128], BF16, kind="Internal").ap()

    with ExitStack() as st:
        kb = KB(nc, st)
        pp = PsumPool(kb)
        outdeps = []
        ones = kb.sb([128, 128], BF16, name="ones")
        dones = Dep()
        kb.op("pool", lambda e: e.memset(ones[:], 1.0), writes=[dones])

        def small(name, shape, dt, src):
            t = kb.sb(shape, dt, name=name)
            d = Dep()
            kb.dma("sp", t[:], src, writes=[d])
            return t, d
        gmix, dgm = small("gmix", [128, KC], F32, gmix_d)
        fb, dfb = small("fb", [128, 1], F32, fb_d)
        gn, dgn = small("gn", [128, 1], F32, gn_d)
        kw, dkw = small("kw", [128, 1], F32, kw_d)
        cd, dcd = small("cd", [128, 1], F32, cd_d)
        qw, dqw = small("qw", [128, 512], F32, qw_d)
        dmat, ddm = small("dmat", [128, 512], F32, dmat_d)
        mu, dmu = small("mu", [128, 4, 512], BF16, mu_d)
        tri, dtri = small("tri", [128, 3, 128], F32, tri_d)

        zf = kb.sb([128, NT], F32, name="zf")
        dzf = Dep()
        lf = kb.sb([128, NT], F32, name="lf")
        dlf = Dep()
        cs = kb.sb([128, NT], F32, name="cs")
        dcs = Dep()
        cbc = kb.sb([128, NT], F32, name="cbc")
        dcbc = Dep()
        totT = kb.sb([128, 128], F32, name="totT")
        dtot = Dep()
        onesf = kb.sb([128, 128], F32, name="onesf")
        donf = Dep()
        with ExitStack() as st1:
            kb.stack = st1
            wl = WLoader(kb, nstage=2)
            wf = kb.sb([128, 5, KC, 128], BF16, name="wf")
            dwf = Dep()
            wt = kb.sb([128, KC, 385], BF16, name="wt")
            dwt = Dep()
            for gi in range(5):
                load_weights_scaled(kb, wl, wf[:, gi], dwf, wf_d[:, gi], gmix, dgm, 128)
            load_weights_scaled(kb, wl, wt, dwt, wt_d, gmix, dgm, 385)
            xring = Ring(kb, 8, [128, 512], F32, "xr")
            hring = Ring(kb, 2, [128, KC, 512], BF16, "hr")
            sqring = Ring(kb, 16, [128, 512], BF16, "sqr")
            tdeps = {n: [Dep() for _ in range(NT)] for n in ("rkw", "rv", "fv")}

            def fl_cb(t, pf, dpf, rcol, drc):
                kb.op("dve", lambda e: e.tensor_scalar(zf[:, t:t + 1], pf[:, 0:1], rcol, fb[:], op0=ALU.mult, op1=ALU.add),
                      reads=[dpf, drc, dfb], writes=[dzf])
            t_out = [(0, 128, rkw_d, tdeps["rkw"], kw[:], dkw), (128, 128, rv_d, tdeps["rv"], None, None),
                     (256, 128, fv_d, tdeps["fv"], None, None)]
            fdeps = projection_pass(kb, pp, S, xT, wf, dwf, 5, wt, dwt, 384, fT, t_out, ones, dones, fl_cb=fl_cb,
                                    xring=xring, hring=hring, sqring=sqring)
            kb.op("act", lambda e: e.activation(lf[:], zf[:], AF.Exp, scale=-1.0), reads=[dzf], writes=[dlf])
            kb.op("act", lambda e: e.activation(lf[:], lf[:], AF.Ln, bias=1.0), reads=[dlf], writes=[dlf])
            kb.op("dve", lambda e: e.tensor_scalar(lf[:], lf[:], -1.0, None, op0=ALU.mult), reads=[dlf], writes=[dlf])
            kb.op("pool", lambda e: e.memset(onesf[:], 1.0), writes=[donf])
            p1, dp1 = pp.get()
            kb.op("pe", lambda e: e.matmul(p1[0:NT, 0:128], lf[:, 0:NT], onesf[:], start=True, stop=True),
                  reads=[dlf, donf], writes=[dp1])
            kb.op("dve", lambda e: e.memset(totT[:], 0.0), writes=[dtot])
            kb.op("dve", lambda e: e.tensor_copy(totT[0:NT, :], p1[0:NT, 0:128]), reads=[dp1], writes=[dtot])
            p2, dp2 = pp.get()
            kb.op("pe", lambda e: e.matmul(p2[:, 0:NT], tri[:, 0, :], lf[:, 0:NT], start=True, stop=False),
                  reads=[dlf, dtri], writes=[dp2])
            kb.op("pe", lambda e: e.matmul(p2[:, 0:NT], totT[:], tri[:, 1, 0:NT], start=False, stop=True),
                  reads=[dtot, dtri], writes=[dp2])
            kb.op("dve", lambda e: e.tensor_copy(cs[:], p2[:, 0:NT]), reads=[dp2], writes=[dcs])
            p3, dp3 = pp.get()
            kb.op("pe", lambda e: e.matmul(p3[:, 0:NT], tri[:, 2, :], cs[:], start=True, stop=True),
                  reads=[dcs, dtri], writes=[dp3])
            kb.op("dve", lambda e: e.tensor_copy(cbc[:], p3[:, 0:NT]), reads=[dp3], writes=[dcbc])
            kb.barrier()
        kb.stack = st

        with ExitStack() as st2:
            kb.stack = st2
            state = kb.sb([128, 128], F32, name="state")
            dst_ = Dep()
            kb.op("dve", lambda e: e.memset(state[:], 0.0), writes=[dst_])
            prevb = Ring(kb, 8, [128, 128], BF16, "prev")
            inr = {n: Ring(kb, 2, [128, 512], BF16, "ri_" + n) for n in ("q", "k", "g", "kw", "v")}
            qwr = Ring(kb, 2, [128, 512], BF16, "qwr")
            attr = Ring(kb, 2, [128, 512], BF16, "attr")
            f32r = Ring(kb, 6, [128, 512], F32, "f32r")
            b16r = Ring(kb, 4, [128, 512], BF16, "b16r")
            onesd = kb.sb([128, 128], BF16, name="onesd")
            dond = Dep()
            kb.op("pool", lambda e: e.memset(onesd[:], 1.0 / 128.0), writes=[dond])
            for g in range(NG):
                t0 = g * 512
                tl = {}
                for n, src, dl in (("q", fT[0], fdeps[0][g]), ("k", fT[1], fdeps[1][g]), ("g", fT[2], fdeps[2][g])):
                    t_, d_ = inr[n].get()
                    kb.dma("sp", t_[:], src[:, t0:t0 + 512], reads=[dl], writes=[d_])
                    tl[n] = (t_, d_)
                for n, src, dl in (("kw", rkw_d, tdeps["rkw"]), ("v", rv_d, tdeps["rv"])):
                    t_, d_ = inr[n].get()
                    kb.dma("act", t_[:].rearrange("p (a b) -> p a b", a=4),
                           src[t0:t0 + 512, :].rearrange("(a p) b -> p a b", p=128),
                           reads=dl[g * 4:g * 4 + 4], writes=[d_])
                    tl[n] = (t_, d_)
                q_, dq_ = tl["q"]
                k_, dk_ = tl["k"]
                g_, dg_ = tl["g"]
                kw_, dkw_ = tl["kw"]
                v_, dv_ = tl["v"]
                pu, dpu = pp.get()
                for n in range(4):
                    sl = slice(n * 128, (n + 1) * 128)
                    kb.op("pe", lambda e: e.matmul(pu[:, sl], kw_[:, sl], v_[:, sl], start=True, stop=True),
                          reads=[dkw_, dv_], writes=[dpu])
                prevs = []
                for n in range(4):
                    sl = slice(n * 128, (n + 1) * 128)
                    pb, dpb = prevb.get()
                    kb.op("dve", lambda e: e.tensor_copy(pb[:], state[:]), reads=[dst_], writes=[dpb])
                    kb.op("dve", lambda e: e.scalar_tensor_tensor(state[:], state[:], cd[:], pu[:, sl],
                                                                  op0=ALU.mult, op1=ALU.add),
                          reads=[dst_, dcd, dpu], writes=[dst_])
                    prevs.append((pb, dpb))
                pa, dpa = pp.get()
                for n in range(4):
                    sl = slice(n * 128, (n + 1) * 128)
                    kb.op("pe", lambda e: e.matmul(pa[:, sl], k_[:, sl], q_[:, sl], start=True, stop=True),
                          reads=[dk_, dq_], writes=[dpa])
                at_, dat_ = attr.get()
                kb.op("dve", lambda e: e.tensor_tensor(at_[:], pa[:], dmat[:], op=ALU.mult),
                      reads=[dpa, ddm], writes=[dat_])
                qw_, dqw_ = qwr.get()
                kb.op("pool", lambda e: e.tensor_tensor(qw_[:], q_[:], qw[:], op=ALU.mult),
                      reads=[dq_, dqw], writes=[dqw_])
                py, dpy = pp.get()
                for n in range(4):
                    sl = slice(n * 128, (n + 1) * 128)
                    kb.op("pe", lambda e: e.matmul(py[:, sl], v_[:, sl], at_[:, sl], start=True, stop=False),
                          reads=[dv_, dat_], writes=[dpy])
                    kb.op("pe", lambda e: e.matmul(py[:, sl], prevs[n][0][:], qw_[:, sl], start=False, stop=True),
                          reads=[prevs[n][1], dqw_], writes=[dpy])
                yb, dyb = b16r.get()
                kb.op("act", lambda e: e.activation(yb[:], py[:], AF.Copy), reads=[dpy], writes=[dyb])
                y2, dy2 = b16r.get()
                kb.op("act", lambda e: e.activation(y2[:], py[:], AF.Square), reads=[dpy], writes=[dy2])
                pm, dpm = pp.get()
                kb.op("pe", lambda e: e.matmul(pm[:], onesd[:], yb[:], start=True, stop=True),
                      reads=[dond, dyb], writes=[dpm])
                pe2, dpe2 = pp.get()
                kb.op("pe", lambda e: e.matmul(pe2[:], onesd[:], y2[:], start=True, stop=True),
                      reads=[dond, dy2], writes=[dpe2])
                mean, dmean = f32r.get()
                kb.op("act", lambda e: e.activation(mean[:], pm[:], AF.Copy), reads=[dpm], writes=[dmean])
                var, dvar = f32r.get()
                kb.op("dve", lambda e: e.tensor_tensor(var[:], mean[:], mean[:], op=ALU.mult),
                      reads=[dmean], writes=[dvar])
                kb.op("dve", lambda e: e.tensor_tensor(var[:], pe2[:], var[:], op=ALU.subtract),
                      reads=[dpe2, dvar], writes=[dvar])
                kb.op("dve", lambda e: e.tensor_scalar(var[:], var[:], EPS, None, op0=ALU.add), reads=[dvar], writes=[dvar])
                kb.op("dve", lambda e: e.reciprocal(var[:], var[:]), reads=[dvar], writes=[dvar])
                kb.op("act", lambda e: e.activation(var[:], var[:], AF.Sqrt), reads=[dvar], writes=[dvar])
                yc, dyc = f32r.get()
                kb.op("dve", lambda e: e.tensor_tensor(yc[:], py[:], mean[:], op=ALU.subtract),
                      reads=[dpy, dmean], writes=[dyc])
                kb.op("dve", lambda e: e.tensor_tensor(yc[:], yc[:], var[:], op=ALU.mult),
                      reads=[dyc, dvar], writes=[dyc])
                sg, dsg = f32r.get()
                kb.op("act", lambda e: e.activation(sg[:], g_[:], AF.Silu), reads=[dg_], writes=[dsg])
                ob, dob = b16r.get()
                kb.op("dve", lambda e: e.scalar_tensor_tensor(ob[:], yc[:], gn[:], sg[:], op0=ALU.mult, op1=ALU.mult),
                      reads=[dyc, dgn, dsg], writes=[dob])
                do = Dep()
                kb.dma("sp", mixT[0:128, t0:t0 + 512], ob[:], reads=[dob], writes=[do])
                outdeps.append(do)
            kb.barrier()
        kb.stack = st

        with ExitStack() as st3:
            kb.stack = st3
            kT = kb.sb([128, S], BF16, name="kT")
            dkT = Dep()
            vv = kb.sb([128, NT, 128], BF16, name="vv")
            dvv = Dep()
            for g in range(NG):
                kb.dma("sp", kT[:, g * 512:(g + 1) * 512], fT[4][:, g * 512:(g + 1) * 512], reads=[fdeps[4][g]], writes=[dkT])
                kb.dma("act", vv[:, g * 4:(g + 1) * 4, :],
                       fv_d[g * 512:(g + 1) * 512, :].rearrange("(a p) b -> p a b", p=128),
                       reads=tdeps["fv"][g * 4:g * 4 + 4], writes=[dvv])
            bring = Ring(kb, 2, [128, NT], F32, "fbias")
            orr = Ring(kb, 2, [128, 512], F32, "fo_r")
            obr = Ring(kb, 2, [128, 512], BF16, "fo_b")

            def bias_fn(j):
                bt, dbt = bring.get()
                col = min(4 * j + 1, NT - 1)
                kb.op("dve", lambda e: e.tensor_scalar(bt[:], cs[:], -1.0, cbc[:, col:col + 1], op0=ALU.mult, op1=ALU.add),
                      reads=[dcs, dcbc], writes=[dbt])
                return (lambda i: bt[:, i:i + 1]), dbt

            def mult_fn(i, j):
                u = i - 4 * j
                if u >= 0:
                    return mu[:, u, :], dmu
                return None

            def out_fn(j, Ops, dO, Lps, dL):
                r, dr = orr.get()
                kb.op("dve", lambda e: e.reciprocal(r[:], Lps[:]), reads=[dL], writes=[dr])
                ob, dob = obr.get()
                kb.op("dve", lambda e: e.tensor_tensor(ob[:], Ops[:], r[:], op=ALU.mult), reads=[dO, dr], writes=[dob])
                do = Dep()
                kb.dma("sp", mixT[128:256, j * 512:(j + 1) * 512], ob[:], reads=[dob], writes=[do])
                outdeps.append(do)

            attention_pass(kb, pp, S, fT[3], fdeps[3], 0, 128, kT, dkT, vv, dvv, 128 ** -0.5, bias_fn, mult_fn,
                           lambda j: list(range(0, 4 * j + 4)), out_fn, ones, dones, name="fx")
        kb.stack = st
        kb.finish(outdeps)
    return nc


def ret_consts(h):
    lg = math.log1p(-2.0 ** (-5.0 - h))
    p = np.arange(128, dtype=np.float64)
    sc = 128 ** -0.5
    kw = (np.exp(lg * (127 - p)) * sc).astype(np.float32).reshape(128, 1)
    cdv = np.full((128, 1), math.exp(lg * 128), np.float32)
    qw = np.tile(np.exp(lg * (p + 1.0))[None, :], (128, 4)).astype(np.float32)
    rel = p[None, :] - p[:, None]
    dm = np.where(rel >= 0, np.exp(lg * np.maximum(rel, 0)), 0.0) * sc
    dmat = np.tile(dm, (1, 4)).astype(np.float32)
    return kw, cdv, qw, dmat


def causal_mu():
    s = np.arange(128)[:, None]
    q = np.arange(512)[None, :]
    out = np.zeros((128, 4, 512), np.float32)
    for u in range(4):
        delta = q // 128 - u
        out[:, u, :] = np.where(delta > 0, 1.0, np.where(delta == 0, ((q % 128) >= s) * 1.0, 0.0))
    return out.astype(ml_dtypes.bfloat16)


def tri_consts():
    p = np.arange(128)
    U = (p[:, None] <= p[None, :]).astype(np.float32)
    SU = (p[:, None] < p[None, :]).astype(np.float32)
    Sel = np.zeros((128, 128), np.float32)
    Sel[127, :] = 1.0
    return np.ascontiguousarray(np.stack([U, SU, Sel], axis=1))


def prep_phase_a0(c, w_in_even, norm_mix0, ret_gn, fox_fb):
    RW = 1024
    cols = lambda base: slice(base + c * 128, base + (c + 1) * 128)
    w = w_in_even
    rq, rk, rv, rg = w[:, cols(0)], w[:, cols(RW)], w[:, cols(2 * RW)], w[:, cols(3 * RW)]
    fq, fk, fv = w[:, cols(4 * RW)], w[:, cols(5 * RW)], w[:, cols(6 * RW)]
    fl = w[:, 7 * RW + c:7 * RW + c + 1]

    def fm(m):
        return m.reshape(KC, 128, m.shape[1]).transpose(1, 0, 2)
    wf = np.ascontiguousarray(np.stack([fm(rq), fm(rk), fm(rg), fm(fq), fm(fk)], axis=1))
    wt = np.ascontiguousarray(fm(np.concatenate([rk, rv, fv, fl], axis=1)))
    kw, cdv, qw, dmat = ret_consts(c)
    return {
        "wf": wf, "wt": wt,
        "gmix": np.ascontiguousarray(norm_mix0.reshape(KC, 128).T),
        "fb": np.full((128, 1), fox_fb[c], np.float32),
        "gn": np.ascontiguousarray(ret_gn[c * 128:(c + 1) * 128].reshape(128, 1)),
        "kw": kw, "cd": cdv, "qw": qw, "dmat": dmat,
        "mu": causal_mu(), "tri": tri_consts(),
    }


BIGM = 30000.0
SELBIG = 1.0e9
LAMBDA_INIT1 = 0.8 - 0.6 * math.exp(-0.3 * 1)


def build_mult_tiles(kb, G_d, mask_d, negb, dnegb, dst, ddst, ntiles, stg_ring, tmp_ring):
    for t in range(ntiles):
        g_, dg_ = stg_ring.get()
        kb.dma("sp", g_[:], G_d[:, t, :], writes=[dg_])
        m_, dm_ = tmp_ring.get()
        kb.dma("act", m_[:], mask_d[:, t, :], writes=[dm_])
        kb.op("act", lambda e: e.activation(g_[:], g_[:], AF.Exp, bias=negb, scale=1.0),
              reads=[dg_, dnegb], writes=[dg_])
        kb.op("dve", lambda e: e.tensor_tensor(dst[:, t, :], g_[:], m_[:], op=ALU.mult),
              reads=[dg_, dm_], writes=[ddst])


def build_phase_a1(S):
    NT = S // 128
    NG = S // 512
    NS = S // 64
    NSR = min(128, NS)
    NST = (NS + 127) // 128
    NC = (S - 32) // 16 + 1
    NCT = (NC + 127) // 128
    NF = 13
    nc = bass.Bass("TRN2", target_bir_lowering=False)
    xT = nc.dram_tensor("xT", [D, S], F32, kind="ExternalInput").ap()
    wf_d = nc.dram_tensor("wf", [128, NF, KC, 128], F32, kind="ExternalInput").ap()
    wt_d = nc.dram_tensor("wt", [128, KC, 384], F32, kind="ExternalInput").ap()
    gmix_d = nc.dram_tensor("gmix", [128, KC], F32, kind="ExternalInput").ap()
    lam_d = nc.dram_tensor("lam", [128, 256], F32, kind="ExternalInput").ap()
    subln_d = nc.dram_tensor("subln", [128, 1], F32, kind="ExternalInput").ap()
    b31_d = nc.dram_tensor("b31", [128, 6], F32, kind="ExternalInput").ap()
    tg_diff_d = nc.dram_tensor("tg_diff", [128, 5, 512], F32, kind="ExternalInput").ap()
    tg_nsa_d = nc.dram_tensor("tg_nsa", [128, 5, 512], F32, kind="ExternalInput").ap()
    tm_d = nc.dram_tensor("tm", [128, 5, 512], BF16, kind="ExternalInput").ap()
    tg_cmp_d = nc.dram_tensor("tg_cmp", [128, 20, 512], F32, kind="ExternalInput").ap()
    tm_cmp_d = nc.dram_tensor("tm_cmp", [128, 5, 512], BF16, kind="ExternalInput").ap()
    wlow_d = nc.dram_tensor("wlow", [128, 4, 512], BF16, kind="ExternalInput").ap()
    ovx_d = nc.dram_tensor("ovx", [128, NCT, NS + 1], BF16, kind="ExternalInput").ap()
    exs_d = nc.dram_tensor("exs", [128, 64, 128], BF16, kind="ExternalInput").ap()
    ident_d = nc.dram_tensor("ident", [128, 128], BF16, kind="ExternalInput").ap()
    cw1_d = nc.dram_tensor("cw1", [2, 128, 32, 128], F32, kind="ExternalInput").ap()
    cw2_d = nc.dram_tensor("cw2", [2, 128, 128], F32, kind="ExternalInput").ap()
    cpos_d = nc.dram_tensor("cpos", [2, 128, 32], F32, kind="ExternalInput").ap()
    mixT = nc.dram_tensor("mixT", [256, S], BF16, kind="ExternalOutput").ap()
    fnames = ["dqT", "dkT", "nq0T", "nq1T", "nq2T", "nq3T", "ckT", "cvT", "skT", "wkT", "g0T", "g1T", "g2T"]
    fT = [nc.dram_tensor("s_" + n, [128, S], BF16, kind="Internal").ap() for n in fnames]
    dv_d = nc.dram_tensor("s_dv", [S, 128], BF16, kind="Internal").ap()
    sv_d = nc.dram_tensor("s_sv", [S, 128], BF16, kind="Internal").ap()
    wv_d = nc.dram_tensor("s_wv", [S, 128], BF16, kind="Internal").ap()
    o1_d = nc.dram_tensor("s_o1", [128, S], F32, kind="Internal").ap()
    ocmp_d = nc.dram_tensor("s_ocmp", [128, S], F32, kind="Internal").ap()
    oslc_d = nc.dram_tensor("s_oslc", [128, S], F32, kind="Internal").ap()
    selT_d = nc.dram_tensor("s_selT", [NST, 128, S], BF16, kind="Internal").ap()

    with ExitStack() as st:
        kb = KB(nc, st)
        pp = PsumPool(kb)
        outdeps = []
        ones = kb.sb([128, 128], BF16, name="ones")
        dones = Dep()
        kb.op("pool", lambda e: e.memset(ones[:], 1.0), writes=[dones])
        onesd = kb.sb([128, 128], BF16, name="onesd")
        dond = Dep()
        kb.op("pool", lambda e: e.memset(onesd[:], 1.0 / 128.0), writes=[dond])

        def small(name, shape, dt, src, q="sp"):
            t = kb.sb(shape, dt, name=name)
            d = Dep()
            kb.dma(q, t[:], src, writes=[d])
            return t, d
        gmix, dgm = small("gmix", [128, KC], F32, gmix_d)
        lam, dlam = small("lam", [128, 256], F32, lam_d)
        subln, dsub = small("subln", [128, 1], F32, subln_d)
        b31, db31 = small("b31", [128, 6], F32, b31_d)
        ident, did = small("ident", [128, 128], BF16, ident_d)
        negb = kb.sb([128, 6], F32, name="negb")
        dnegb = Dep()
        kb.op("dve", lambda e: e.tensor_scalar(negb[:], b31[:], -1.0, None, op0=ALU.mult), reads=[db31], writes=[dnegb])
        lt = kb.sb([128, 128], F32, name="lt")
        dlt = Dep()
        l2 = kb.sb([128, 4], F32, name="l2")
        dl2 = Dep()
        kb.op("dve", lambda e: e.tensor_tensor(lt[:, 0:64], lam[:, 0:64], lam[:, 64:128], op=ALU.mult), reads=[dlam], writes=[dlt])
        kb.op("dve", lambda e: e.tensor_tensor(lt[:, 64:128], lam[:, 128:192], lam[:, 192:256], op=ALU.mult), reads=[dlam, dlt], writes=[dlt])
        kb.op("dve", lambda e: e.tensor_reduce(l2[:, 0:1], lt[:, 0:64], axis=AX.X, op=ALU.add), reads=[dlt], writes=[dl2])
        kb.op("dve", lambda e: e.tensor_reduce(l2[:, 1:2], lt[:, 64:128], axis=AX.X, op=ALU.add), reads=[dlt, dl2], writes=[dl2])
        kb.op("act", lambda e: e.activation(l2[:, 0:2], l2[:, 0:2], AF.Exp), reads=[dl2], writes=[dl2])
        kb.op("dve", lambda e: e.tensor_tensor(l2[:, 2:3], l2[:, 1:2], l2[:, 0:1], op=ALU.subtract), reads=[dl2], writes=[dl2])
        kb.op("dve", lambda e: e.tensor_scalar(l2[:, 3:4], l2[:, 2:3], -LAMBDA_INIT1, None, op0=ALU.add), reads=[dl2], writes=[dl2])
        neglam = l2[:, 3:4]
        sgain = kb.sb([128, 1], F32, name="sgain")
        dsg_ = Dep()
        kb.op("dve", lambda e: e.tensor_scalar(sgain[:], subln[:], 1.0 - LAMBDA_INIT1, None, op0=ALU.mult), reads=[dsub], writes=[dsg_])

        with ExitStack() as st1:
            kb.stack = st1
            wl = WLoader(kb, nstage=2)
            wf = kb.sb([128, NF, KC, 128], BF16, name="wf")
            dwf = Dep()
            wt = kb.sb([128, KC, 384], BF16, name="wt")
            dwt = Dep()
            for gi in range(NF):
                load_weights_scaled(kb, wl, wf[:, gi], dwf, wf_d[:, gi], gmix, dgm, 128)
            load_weights_scaled(kb, wl, wt, dwt, wt_d, gmix, dgm, 384)
            xring = Ring(kb, 8, [128, 512], F32, "xr")
            hring = Ring(kb, 2, [128, KC, 512], BF16, "hr")
            sqring = Ring(kb, 16, [128, 512], BF16, "sqr")
            tdeps = {n: [Dep() for _ in range(NT)] for n in ("dv", "sv", "wv")}
            t_out = [(0, 128, dv_d, tdeps["dv"], None, None), (128, 128, sv_d, tdeps["sv"], None, None),
                     (256, 128, wv_d, tdeps["wv"], None, None)]
            fdeps = projection_pass(kb, pp, S, xT, wf, dwf, NF, wt, dwt, 384, fT, t_out, ones, dones,
                                    xring=xring, hring=hring, sqring=sqring)
            kb.barrier()
        kb.stack = st

        def load_kv(kT, dkT, vv, dvv, fk, fkd, vd, vdeps):
            for g in range(NG):
                kb.dma("sp", kT[:, g * 512:(g + 1) * 512], fk[:, g * 512:(g + 1) * 512], reads=[fkd[g]], writes=[dkT])
                kb.dma("act", vv[:, g * 4:(g + 1) * 4, :],
                       vd[g * 512:(g + 1) * 512, :].rearrange("(a p) b -> p a b", p=128),
                       reads=vdeps[g * 4:g * 4 + 4], writes=[dvv])

        def t5_mult(M, dM):
            def f(i, j):
                u = i - 4 * j
                if u >= -1:
                    return M[:, u + 1, :], dM
                return None
            return f

        with ExitStack() as st2:
            kb.stack = st2
            kT = kb.sb([128, S], BF16, name="kT")
            dkT = Dep()
            vv = kb.sb([128, NT, 128], BF16, name="vv")
            dvv = Dep()
            load_kv(kT, dkT, vv, dvv, fT[1], fdeps[1], dv_d, tdeps["dv"])
            stg = Ring(kb, 2, [128, 512], F32, "stg")
            tmpb = Ring(kb, 2, [128, 512], BF16, "tmpb")
            Md = kb.sb([128, 5, 512], BF16, name="Md")
            dMd = Dep()
            build_mult_tiles(kb, tg_diff_d, tm_d, negb[:, 0:1], dnegb, Md, dMd, 5, stg, tmpb)
            f32r = Ring(kb, 4, [128, 512], F32, "d_f32")
            b16r = Ring(kb, 3, [128, 512], BF16, "d_b16")
            o1deps = [Dep() for _ in range(NG)]

            def out0(j, Ops, dO, Lps, dL):
                r, dr = f32r.get()
                kb.op("dve", lambda e: e.reciprocal(r[:], Lps[:]), reads=[dL], writes=[dr])
                o, do_ = f32r.get()
                kb.op("dve", lambda e: e.tensor_tensor(o[:], Ops[:], r[:], op=ALU.mult), reads=[dO, dr], writes=[do_])
                kb.dma("sp", o1_d[:, j * 512:(j + 1) * 512], o[:], reads=[do_], writes=[o1deps[j]])

            def out1(j, Ops, dO, Lps, dL):
                r, dr = f32r.get()
                kb.op("dve", lambda e: e.reciprocal(r[:], Lps[:]), reads=[dL], writes=[dr])
                o2, do2 = f32r.get()
                kb.op("dve", lambda e: e.tensor_tensor(o2[:], Ops[:], r[:], op=ALU.mult), reads=[dO, dr], writes=[do2])
                o1, do1 = f32r.get()
                kb.dma("sp", o1[:], o1_d[:, j * 512:(j + 1) * 512], reads=[o1deps[j]], writes=[do1])
                kb.op("dve", lambda e: e.scalar_tensor_tensor(o2[:], o2[:], neglam, o1[:], op0=ALU.mult, op1=ALU.add),
                      reads=[do2, dl2, do1], writes=[do2])
                sq, dsq = b16r.get()
                kb.op("act", lambda e: e.activation(sq[:], o2[:], AF.Square), reads=[do2], writes=[dsq])
                pm, dpm = pp.sub([7]).get()
                kb.op("pe", lambda e: e.matmul(pm[:], onesd[:], sq[:], start=True, stop=True), reads=[dond, dsq], writes=[dpm])
                rstd_from_ss(kb, pm[:], dpm, r[:], dr, 512, inv_d=1.0)
                ob, dob = b16r.get()
                kb.op("dve", lambda e: e.scalar_tensor_tensor(ob[:], o2[:], sgain[:], r[:], op0=ALU.mult, op1=ALU.mult),
                      reads=[do2, dsg_, dr], writes=[dob])
                do = Dep()
                kb.dma("sp", mixT[0:128, j * 512:(j + 1) * 512], ob[:], reads=[dob], writes=[do])
                outdeps.append(do)

            for m, ofn in ((0, out0), (1, out1)):
                attention_pass(kb, pp, S, fT[0], fdeps[0], 64 * m, 64, kT, dkT, vv, dvv, 64 ** -0.5,
                               lambda j: (b31[:, 0:1], db31), t5_mult(Md, dMd),
                               lambda j: list(range(0, 4 * j + 4)), ofn, ones, dones, name="df%d" % m)
            kb.barrier()
        kb.stack = st

        with ExitStack() as st3:
            kb.stack = st3
            kcT = kb.sb([128, NCT * 128], BF16, name="kcT")
            dkc = Dep()
            vc = kb.sb([128, NCT, 128], BF16, name="vc")
            dvc = Dep()
            kb.op("pool", lambda e: e.memset(kcT[:], 0.0), writes=[dkc])
            kb.op("pool", lambda e: e.memset(vc[:], 0.0), writes=[dvc])
            with ExitStack() as st3a:
                kb.stack = st3a
                wl = WLoader(kb, nstage=2)
                kvT = kb.sb([128, S], BF16, name="kvT")
                dkv = Dep()
                w1b = kb.sb([128, 32, 128], BF16, name="w1b")
                dw1 = Dep()
                w2b = kb.sb([128, 128], BF16, name="w2b")
                dw2 = Dep()
                posb = kb.sb([128, 32], BF16, name="posb")
                dpos = Dep()
                bcol = kb.sb([128, 1], F32, name="bcol")
                dbc = Dep()
                hidr = Ring(kb, 2, [128, 512], BF16, "hid")
                for kv in range(2):
                    src = fT[6 + kv]
                    for g in range(NG):
                        kb.dma("sp", kvT[:, g * 512:(g + 1) * 512], src[:, g * 512:(g + 1) * 512],
                               reads=[fdeps[6 + kv][g]], writes=[dkv])
                    for l0 in range(0, 32, 16):
                        wl.load(w1b[:, l0:l0 + 16, :].rearrange("p a b -> p (a b)"), dw1,
                                cw1_d[kv, :, l0:l0 + 16, :].rearrange("p a b -> p (a b)"), 2048)
                    wl.load(w2b[:], dw2, cw2_d[kv], 128)
                    wl.load(posb[:], dpos, cpos_d[kv], 32)
                    pb_, dpb_ = pp.get()
                    for l in range(32):
                        kb.op("pe", lambda e: e.matmul(pb_[:, 0:1], w1b[:, l, :], posb[:, l:l + 1],
                                                       start=(l == 0), stop=(l == 31)),
                              reads=[dw1, dpos], writes=[dpb_], inc=(l == 31))
                    kb.op("dve", lambda e: e.tensor_copy(bcol[:], pb_[:, 0:1]), reads=[dpb_], writes=[dbc])
                    for n0 in range(0, NC, 512):
                        nn = min(512, NC - n0)
                        ph, dph = pp.get()
                        for l in range(32):
                            a0 = 16 * n0 + l
                            kb.op("pe", lambda e: e.matmul(ph[:, 0:nn], w1b[:, l, :], kvT[:, a0:a0 + 16 * (nn - 1) + 1:16],
                                                           start=(l == 0), stop=(l == 31)),
                                  reads=[dw1, dkv], writes=[dph], inc=(l == 31))
                        hid, dhid = hidr.get()
                        kb.op("act", lambda e: e.activation(hid[:, 0:nn], ph[:, 0:nn], AF.Silu, bias=bcol[:], scale=1.0),
                              reads=[dph, dbc], writes=[dhid])
                        if kv == 0:
                            pk, dpk = pp.get()
                            kb.op("pe", lambda e: e.matmul(pk[:, 0:nn], w2b[:], hid[:, 0:nn], start=True, stop=True),
                                  reads=[dw2, dhid], writes=[dpk])
                            kb.op("dve", lambda e: e.tensor_copy(kcT[:, n0:n0 + nn], pk[:, 0:nn]), reads=[dpk], writes=[dkc])
                        else:
                            for b0 in range(0, nn, 128):
                                bn = min(128, nn - b0)
                                pv_, dpv_ = pp.get()
                                kb.op("pe", lambda e: e.matmul(pv_[0:bn, 0:128], hid[:, b0:b0 + bn], w2b[:], start=True, stop=True),
                                      reads=[dw2, dhid], writes=[dpv_])
                                kb.op("dve", lambda e: e.tensor_copy(vc[0:bn, (n0 + b0) // 128, :], pv_[0:bn, 0:128]),
                                      reads=[dpv_], writes=[dvc])
                kb.barrier()
            kb.stack = st3
            stg = Ring(kb, 2, [128, 512], F32, "stg3")
            tmpb = Ring(kb, 2, [128, 512], BF16, "tmpb3")
            Mc = kb.sb([128, 20, 512], BF16, name="Mc")
            dMc = Dep()
            for hh in range(4):
                build_mult_tiles(kb, tg_cmp_d[:, hh * 5:(hh + 1) * 5, :], tm_cmp_d, negb[:, 2 + hh:3 + hh], dnegb,
                                 Mc[:, hh * 5:(hh + 1) * 5, :], dMc, 5, stg, tmpb)
            ovx, dovx = small("ovx", [128, NCT, NS + 1], BF16, ovx_d)
            qring = Ring(kb, 3, [128, 512], BF16, "cq")
            PT = [Ring(kb, NCT, [128, 512], BF16, "cp%d" % k) for k in range(2)]
            pe_ring = Ring(kb, 3, [128, 512], BF16, "cpe")
            imp = kb.sb([128, 4, NS], F32, name="imp")
            dimp = Dep()
            rl = Ring(kb, 3, [128, 1], F32, "crl")
            m8 = Ring(kb, 4, [128, 8], F32, "cm8")
            wk_ = Ring(kb, 2, [128, NS], F32, "cwk")
            selb = Ring(kb, 2, [128, NS], BF16, "csel")
            selT = Ring(kb, 2, [128, NST, 512], BF16, "cselT")
            f32r = Ring(kb, 3, [128, 512], F32, "c_f32")
            gring = Ring(kb, 2, [128, 512], BF16, "c_g")
            ocdeps = [Dep() for _ in range(NG)]
            sdeps = [Dep() for _ in range(NG)]
            pacc = pp.sub([0, 1])
            pst = pp.sub([2, 3, 4])
            pg_ = pp.sub([5, 6])
            ptr = pp.sub([7])
            citems = []
            for j in range(NG):
                nmax = min(NC - 1, (512 * j + 480) // 16)
                its = list(range(0, nmax // 128 + 1))
                for hh in range(4):
                    for n_i, i in enumerate(its):
                        citems.append((j, hh, n_i, i, len(its)))
            cst = {}
            cq = {}
            cacc = {}

            def c_issue(t):
                j, hh, n_i, i, nk = citems[t]
                if n_i == 0:
                    qt, dqt = qring.get()
                    kb.dma("sp", qt[:], fT[2 + hh][:, j * 512:(j + 1) * 512], reads=[fdeps[2 + hh][j]], writes=[dqt])
                    cq[(j, hh)] = (qt, dqt)
                qt, dqt = cq[(j, hh)]
                sT, dsT = pst.get()
                kb.op("pe", lambda e: e.matmul(sT[:], kcT[:, i * 128:(i + 1) * 128], qt[:], start=True, stop=True),
                      reads=[dkc, dqt], writes=[dsT])
                o = j - 4 * i
                pT_, dpT_ = PT[hh % 2].tiles[i], PT[hh % 2].deps[i]
                if o <= 4:
                    pe_, dpe_ = pe_ring.get()
                    kb.op("act", lambda e: e.activation(pe_[:], sT[:], AF.Exp, bias=b31[:, 2 + hh:3 + hh], scale=128 ** -0.5),
                          reads=[dsT, db31], writes=[dpe_])
                    kb.op("dve", lambda e: e.tensor_tensor(pT_[:], pe_[:], Mc[:, hh * 5 + o, :], op=ALU.mult),
                          reads=[dpe_, dMc], writes=[dpT_])
                else:
                    kb.op("act", lambda e: e.activation(pT_[:], sT[:], AF.Exp, bias=b31[:, 2 + hh:3 + hh], scale=128 ** -0.5),
                          reads=[dsT, db31], writes=[dpT_])
                cst[t] = (pT_, dpT_)

            def c_post(t):
                j, hh, n_i, i, nk = citems[t]
                pT_, dpT_ = cst.pop(t)
                if hh == 0:
                    if n_i == 0:
                        cacc["O"], cacc["dO"] = pacc.get()
                        cacc["L"], cacc["dL"] = pacc.get()
                    first, last = (n_i == 0), (n_i == nk - 1)
                    kb.op("pe", lambda e: e.matmul(cacc["O"][:], vc[:, i, :], pT_[:], start=first, stop=last),
                          reads=[dvc, dpT_], writes=[cacc["dO"]])
                    kb.op("pe", lambda e: e.matmul(cacc["L"][:], ones[:], pT_[:], start=first, stop=last),
                          reads=[dones, dpT_], writes=[cacc["dL"]])
                if n_i != nk - 1:
                    return
                its = [citems[t - (nk - 1) + k][3] for k in range(nk)]
                for b in range(4):
                    G, dG = pg_.get()
                    for k_i, ii in enumerate(its):
                        pa, dpa_ = PT[hh % 2].tiles[ii], PT[hh % 2].deps[ii]
                        kb.op("pe", lambda e: e.matmul(G[:, 0:NS + 1], pa[:, b * 128:(b + 1) * 128], ovx[:, ii, :],
                                                       start=(k_i == 0), stop=(k_i == nk - 1)),
                              reads=[dpa_, dovx], writes=[dG])
                    r_, dr_ = rl.get()
                    kb.op("dve", lambda e: e.tensor_scalar(r_[:], G[:, NS:NS + 1], 1e-30, None, op0=ALU.max),
                          reads=[dG], writes=[dr_])
                    kb.op("dve", lambda e: e.reciprocal(r_[:], r_[:]), reads=[dr_], writes=[dr_])
                    if hh == 0:
                        kb.op("dve", lambda e: e.tensor_scalar(imp[:, b, :], G[:, 0:NS], r_[:], None, op0=ALU.mult),
                              reads=[dG, dr_], writes=[dimp])
                    else:
                        kb.op("dve", lambda e: e.scalar_tensor_tensor(imp[:, b, :], G[:, 0:NS], r_[:], imp[:, b, :],
                                                                      op0=ALU.mult, op1=ALU.add),
                              reads=[dG, dr_, dimp], writes=[dimp])
                if hh == 0:
                    Ops, dO, Lps, dL = cacc["O"], cacc["dO"], cacc["L"], cacc["dL"]
                    r, dr = f32r.get()
                    kb.op("dve", lambda e: e.tensor_scalar(r[:], Lps[:], 1e-30, None, op0=ALU.max), reads=[dL], writes=[dr])
                    kb.op("dve", lambda e: e.reciprocal(r[:], r[:]), reads=[dr], writes=[dr])
                    oc, doc = f32r.get()
                    kb.op("dve", lambda e: e.tensor_tensor(oc[:], Ops[:], r[:], op=ALU.mult), reads=[dO, dr], writes=[doc])
                    gt, dgt = gring.get()
                    kb.dma("sp", gt[:], fT[10][:, j * 512:(j + 1) * 512], reads=[fdeps[10][j]], writes=[dgt])
                    kb.op("act", lambda e: e.activation(r[:], gt[:], AF.Sigmoid), reads=[dgt, dr], writes=[dr])
                    kb.op("dve", lambda e: e.tensor_tensor(oc[:], oc[:], r[:], op=ALU.mult), reads=[doc, dr], writes=[doc])
                    kb.dma("sp", ocmp_d[:, j * 512:(j + 1) * 512], oc[:], reads=[doc], writes=[ocdeps[j]])
                if hh != 3:
                    return
                sT_, dsT_ = selT.get()
                for b in range(4):
                    B = 4 * j + b
                    sc = imp[:, b, :]
                    if 2 * B + 1 < NS:
                        kb.op("dve", lambda e: e.memset(imp[0:64, b, 2 * B + 1:NS], -SELBIG), reads=[dimp], writes=[dimp])
                    if 2 * B + 2 < NS:
                        kb.op("dve", lambda e: e.memset(imp[64:128, b, 2 * B + 2:NS], -SELBIG), reads=[dimp], writes=[dimp])
                    kb.op("dve", lambda e: e.memset(imp[:, b, 0:1], SELBIG), reads=[dimp], writes=[dimp])
                    kb.op("dve", lambda e: e.memset(imp[0:64, b, max(2 * B - 1, 0):2 * B + 1], SELBIG), reads=[dimp], writes=[dimp])
                    kb.op("dve", lambda e: e.memset(imp[64:128, b, 2 * B:2 * B + 2], SELBIG), reads=[dimp], writes=[dimp])
                    m1, dm1 = m8.get()
                    kb.op("dve", lambda e: e.max(m1[:], sc), reads=[dimp], writes=[dm1])
                    w_, dw_ = wk_.get()
                    kb.op("dve", lambda e: e.match_replace(w_[:], m1[:], sc, -3.0e9), reads=[dimp, dm1], writes=[dw_])
                    m2, dm2 = m8.get()
                    kb.op("dve", lambda e: e.max(m2[:], w_[:]), reads=[dw_], writes=[dm2])
                    sb_, dsb_ = selb.get()
                    kb.op("dve", lambda e: e.tensor_scalar(sb_[:], sc, m2[:, 7:8], 1.0, op0=ALU.is_ge, op1=ALU.subtract),
                          reads=[dimp, dm2], writes=[dsb_])
                    for jt in range(NST):
                        tp, dtp = ptr.get()
                        tpb = tp[:].bitcast(BF16)
                        kb.op("pe", lambda e: e.transpose(tpb[0:NSR, 0:128], sb_[:, jt * 128:jt * 128 + NSR], ident[:]),
                              reads=[dsb_, did], writes=[dtp])
                        kb.op("act", lambda e: e.activation(sT_[0:NSR, jt, b * 128:(b + 1) * 128], tpb[0:NSR, 0:128], AF.Copy),
                              reads=[dtp], writes=[dsT_])
                for jt in range(NST):
                    kb.dma("sp", selT_d[jt, 0:NSR, j * 512:(j + 1) * 512], sT_[0:NSR, jt, :], reads=[dsT_], writes=[sdeps[j]])

            CLOOK = 2
            for t in range(len(citems) + CLOOK):
                if t - CLOOK >= 0:
                    c_post(t - CLOOK)
                if t < len(citems):
                    c_issue(t)
            kb.barrier()
        kb.stack = st

        with ExitStack() as st4:
            kb.stack = st4
            kT = kb.sb([128, S], BF16, name="kT4")
            dkT = Dep()
            vv = kb.sb([128, NT, 128], BF16, name="vv4")
            dvv = Dep()
            stg = Ring(kb, 2, [128, 512], F32, "stg4")
            tmpb = Ring(kb, 2, [128, 512], BF16, "tmpb4")
            Mn = kb.sb([128, 5, 512], BF16, name="Mn")
            dMn = Dep()
            build_mult_tiles(kb, tg_nsa_d, tm_d, negb[:, 1:2], dnegb, Mn, dMn, 5, stg, tmpb)
            exs, dexs = small("exs", [128, 64, 128], BF16, exs_d)
            wlow, dwlow = small("wlow", [128, 4, 512], BF16, wlow_d, q="act")
            Mw = kb.sb([128, 512], BF16, name="Mw")
            dMw = Dep()
            kb.op("dve", lambda e: e.tensor_tensor(Mw[:], Mn[:, 0, :], wlow[:, 3, :], op=ALU.mult), reads=[dMn, dwlow], writes=[dMw])
            selr = Ring(kb, 2, [128, NST, 512], BF16, "selr")
            f32r = Ring(kb, 5, [128, 512], F32, "s_f32")
            gring = Ring(kb, 2, [128, 512], BF16, "s_g")
            b16r = Ring(kb, 2, [128, 512], BF16, "s_b16")
            osdeps = [Dep() for _ in range(NG)]
            cur_sel = {}

            def pre_slc(j):
                t_, d_ = selr.get()
                for jt in range(NST):
                    kb.dma("sp", t_[0:NSR, jt, :], selT_d[jt, 0:NSR, j * 512:(j + 1) * 512], reads=[sdeps[j]], writes=[d_])
                cur_sel["t"] = (t_, d_)

            def extra_slc(i, j):
                t_, d_ = cur_sel["t"]
                jt = (2 * i) // 128
                k = ((2 * i) % 128) // 2
                return exs[0:NSR, k, :], t_[0:NSR, jt, :], [dexs, d_]

            def gated(j, Ops, dO, Lps, dL, gidx):
                r, dr = f32r.get()
                kb.op("dve", lambda e: e.reciprocal(r[:], Lps[:]), reads=[dL], writes=[dr])
                o, do_ = f32r.get()
                kb.op("dve", lambda e: e.tensor_tensor(o[:], Ops[:], r[:], op=ALU.mult), reads=[dO, dr], writes=[do_])
                gt, dgt = gring.get()
                kb.dma("act", gt[:], fT[gidx][:, j * 512:(j + 1) * 512], reads=[fdeps[gidx][j]], writes=[dgt])
                kb.op("act", lambda e: e.activation(r[:], gt[:], AF.Sigmoid), reads=[dgt, dr], writes=[dr])
                kb.op("dve", lambda e: e.tensor_tensor(o[:], o[:], r[:], op=ALU.mult), reads=[do_, dr], writes=[do_])
                return o, do_

            def out_slc(j, Ops, dO, Lps, dL):
                o, do_ = gated(j, Ops, dO, Lps, dL, 11)
                kb.dma("sp", oslc_d[:, j * 512:(j + 1) * 512], o[:], reads=[do_], writes=[osdeps[j]])

            def out_win(j, Ops, dO, Lps, dL):
                o, do_ = gated(j, Ops, dO, Lps, dL, 12)
                for src, dl in ((ocmp_d, ocdeps), (oslc_d, osdeps)):
                    a, da_ = f32r.get()
                    kb.dma("sp", a[:], src[:, j * 512:(j + 1) * 512], reads=[dl[j]], writes=[da_])
                    kb.op("dve", lambda e: e.tensor_tensor(o[:], o[:], a[:], op=ALU.add), reads=[do_, da_], writes=[do_])
                ob, dob = b16r.get()
                kb.op("act", lambda e: e.activation(ob[:], o[:], AF.Copy), reads=[do_], writes=[dob])
                do = Dep()
                kb.dma("sp", mixT[128:256, j * 512:(j + 1) * 512], ob[:], reads=[dob], writes=[do])
                outdeps.append(do)

            def mult_win(i, j):
                u = i - 4 * j
                if u >= 0:
                    return Mn[:, u + 1, :], dMn
                if u == -1:
                    return Mw[:], dMw
                return wlow[:, u + 4, :], dwlow

            load_kv(kT, dkT, vv, dvv, fT[8], fdeps[8], sv_d, tdeps["sv"])
            attention_pass(kb, pp, S, fT[2], fdeps[2], 0, 128, kT, dkT, vv, dvv, 128 ** -0.5,
                           lambda j: (b31[:, 1:2], db31), t5_mult(Mn, dMn),
                           lambda j: list(range(0, 4 * j + 4)), out_slc, ones, dones, extra_fn=extra_slc,
                           pre_fn=pre_slc, name="sl")
            load_kv(kT, dkT, vv, dvv, fT[9], fdeps[9], wv_d, tdeps["wv"])
            attention_pass(kb, pp, S, fT[2], fdeps[2], 0, 128, kT, dkT, vv, dvv, 128 ** -0.5,
                           lambda j: (b31[:, 1:2], db31), mult_win,
                           lambda j: list(range(max(0, 4 * j - 4), 4 * j + 4)), out_win, ones, dones, name="wn")
        kb.stack = st
        kb.finish(outdeps)
    return nc


def t5_bucket_np(dist):
    n = np.maximum(dist, 0)
    nf = np.maximum(n, 1).astype(np.float32)
    large = 16 + (np.log(nf / np.float32(16)) / np.float32(math.log(128 / 16)) * np.float32(16)).astype(np.int32)
    large = np.minimum(large, 31)
    return np.where(n < 16, n, large)


def t5_tiles(table_col):
    s = np.arange(128)[:, None]
    q = np.arange(512)[None, :]
    G = np.zeros((128, 5, 512), np.float32)
    M = np.zeros((128, 5, 512), np.float32)
    for ui, u in enumerate(range(-1, 4)):
        dist = (q // 128 - u) * 128 + (q % 128) - s
        valid = dist >= 0
        G[:, ui, :] = np.where(valid, table_col[t5_bucket_np(dist)], table_col[31])
        M[:, ui, :] = valid
    return G, M.astype(ml_dtypes.bfloat16)


def cmp_tiles(table_col):
    n = np.arange(128)[:, None]
    q = np.arange(512)[None, :]
    G = np.zeros((128, 5, 512), np.float32)
    M = np.zeros((128, 5, 512), np.float32)
    for o in range(5):
        dist = 512 * o + q - 16 * n - 31
        valid = dist >= 0
        G[:, o, :] = np.where(valid, table_col[t5_bucket_np(dist)], table_col[31])
        M[:, o, :] = valid
    return G, M.astype(ml_dtypes.bfloat16)


def win_low_tiles():
    s = np.arange(128)[:, None]
    q = np.arange(512)[None, :]
    out = np.zeros((128, 4, 512), np.float32)
    for up in range(4):
        delta = q // 128 + 4 - up
        out[:, up, :] = np.where(delta > 4, 0.0, np.where(delta == 4, ((q % 128) < s) * 1.0, 1.0))
    return out.astype(ml_dtypes.bfloat16)


def a1_static(S):
    NS = S // 64
    NC = (S - 32) // 16 + 1
    NCT = (NC + 127) // 128
    n = np.arange(NCT * 128)
    jj = np.arange(NS)
    ov = ((16 * n[:, None] <= 64 * jj[None, :] + 63) & (16 * n[:, None] + 31 >= 64 * jj[None, :]) & (n[:, None] < NC))
    ovx = np.concatenate([ov.astype(np.float32), np.ones((NCT * 128, 1), np.float32)], axis=1)
    ovx = np.ascontiguousarray(ovx.reshape(NCT, 128, NS + 1).transpose(1, 0, 2)).astype(ml_dtypes.bfloat16)
    exs = np.zeros((128, 64, 128), np.float32)
    for k in range(64):
        exs[2 * k, k, 0:64] = BIGM
        exs[2 * k + 1, k, 64:128] = BIGM
    return {"ovx": ovx, "exs": exs.astype(ml_dtypes.bfloat16), "ident": np.eye(128, dtype=np.float32).astype(ml_dtypes.bfloat16),
            "wlow": win_low_tiles()}


def prep_phase_a1(c, S, w_in_odd, norm_mix1, diff_lambda, diff_subln, cmp_pos, cmp_w1, cmp_w2, t5_table, static=None):
    g, r = c // 4, c % 4
    order = [r] + [k for k in range(4) if k != r]
    w = w_in_odd

    def col(base, idx, n=128):
        return w[:, base + idx * n: base + (idx + 1) * n]

    def fm(m):
        return m.reshape(KC, 128, m.shape[1]).transpose(1, 0, 2)
    groups = [col(0, c), col(1024, c)]
    groups += [w[:, 3072 + g * 512 + k * 128: 3072 + g * 512 + (k + 1) * 128] for k in order]
    groups += [col(4096, g), col(4352, g), col(4608, g), col(5120, g)]
    for jb in range(3):
        cc = 5632 + jb * 8 + c
        groups.append(np.repeat(w[:, cc:cc + 1], 128, axis=1))
    wf = np.ascontiguousarray(np.stack([fm(m) for m in groups], axis=1))
    wt = np.ascontiguousarray(fm(np.concatenate([col(2048, c), col(4864, g), col(5376, g)], axis=1)))
    heads_cols = [c, 8 + c] + [8 + g * 4 + k for k in order]
    b31 = np.ascontiguousarray(np.tile(t5_table[31, heads_cols][None, :], (128, 1)))
    tg_diff, tm = t5_tiles(t5_table[:, c])
    tg_nsa, _ = t5_tiles(t5_table[:, 8 + c])
    tgc = []
    for k in order:
        gk, tmc = cmp_tiles(t5_table[:, 8 + g * 4 + k])
        tgc.append(gk)
    d = dict(static if static is not None else a1_static(S))
    d.update({
        "wf": wf, "wt": wt,
        "gmix": np.ascontiguousarray(norm_mix1.reshape(KC, 128).T),
        "lam": np.ascontiguousarray(np.tile(diff_lambda.reshape(1, 256), (128, 1))),
        "subln": np.ascontiguousarray(diff_subln.reshape(128, 1)),
        "b31": b31, "tg_diff": tg_diff, "tg_nsa": tg_nsa, "tm": tm,
        "tg_cmp": np.ascontiguousarray(np.concatenate(tgc, axis=1)), "tm_cmp": tmc,
        "cw1": np.ascontiguousarray(cmp_w1.reshape(2, 32, 128, 128).transpose(0, 2, 1, 3)),
        "cw2": np.ascontiguousarray(cmp_w2),
        "cpos": np.ascontiguousarray(cmp_pos.transpose(0, 2, 1)),
    })
    return d


SEQ = 16384
_NC_CACHE = {}


def _get_nc(key, fn):
    if key not in _NC_CACHE:
        _NC_CACHE[key] = fn()
    return _NC_CACHE[key]


def _assemble_mix(results):
    S = results[0]["mixT"].shape[1]
    mixT = np.empty((D, S), dtype=ml_dtypes.bfloat16)
    for c in range(NCORES):
        m = np.asarray(results[c]["mixT"])
        mixT[c * 128:(c + 1) * 128] = m[0:128]
        mixT[1024 + c * 128:1024 + (c + 1) * 128] = m[128:256]
    return mixT


def _run_b(xT, mixT, pT, wd, final):
    S = xT.shape[1]
    T = S // NCORES
    nc = _get_nc(("b", T, final), lambda: build_phase_b(T, final))
    ims = []
    for c in range(NCORES):
        d = dict(wd)
        d["xT"] = np.ascontiguousarray(xT[:, c * T:(c + 1) * T])
        d["mixT"] = np.ascontiguousarray(mixT[:, c * T:(c + 1) * T])
        d["pT"] = np.ascontiguousarray(pT[:, c * T:(c + 1) * T])
        ims.append(d)
    res = run_bass_kernel_spmd(nc, ims, core_ids=list(range(NCORES)))
    return np.concatenate([np.asarray(res.results[c]["outT"]) for c in range(NCORES)], axis=1)


def kernel(x, p, norm_mix, norm_ffn, w_in_even, ret_gn, fox_fb, w_out_even, w_in_odd, diff_lambda, diff_subln,
           cmp_pos, cmp_w1, cmp_w2, w_out_odd, t5_table, ffn_gate, ffn_up, ffn_down, ple_gate, ple_proj, final_norm):
    f = lambda a: np.asarray(a, dtype=np.float32)
    x, p = f(x), f(p)
    S = x.shape[1]
    xT = np.ascontiguousarray(x[0].T)
    nc = _get_nc(("a0", S), lambda: build_phase_a0(S))
    ims = []
    for c in range(NCORES):
        d = prep_phase_a0(c, f(w_in_even)[0], f(norm_mix)[0], f(ret_gn)[0], f(fox_fb)[0])
        d["xT"] = xT
        ims.append(d)
    res = run_bass_kernel_spmd(nc, ims, core_ids=list(range(NCORES)))
    mixT = _assemble_mix(res.results)
    wd = prep_phase_b_weights(f(w_out_even)[0], f(ffn_gate)[0], f(ffn_up)[0], f(ffn_down)[0], f(ple_gate)[0],
                              f(ple_proj)[0], f(norm_ffn)[0], f(final_norm))
    xT = _run_b(xT, mixT, np.ascontiguousarray(p[0, 0].T), wd, False)
    nc = _get_nc(("a1", S), lambda: build_phase_a1(S))
    static = a1_static(S)
    ims = []
    for c in range(NCORES):
        d = prep_phase_a1(c, S, f(w_in_odd)[0], f(norm_mix)[1], f(diff_lambda)[0], f(diff_subln)[0], f(cmp_pos)[0],
                          f(cmp_w1)[0], f(cmp_w2)[0], f(t5_table), static)
        d["xT"] = xT
        ims.append(d)
    res = run_bass_kernel_spmd(nc, ims, core_ids=list(range(NCORES)))
    mixT = _assemble_mix(res.results)
    wd = prep_phase_b_weights(f(w_out_odd)[0], f(ffn_gate)[1], f(ffn_up)[1], f(ffn_down)[1], f(ple_gate)[1],
                              f(ple_proj)[1], f(norm_ffn)[1], f(final_norm))
    outT = _run_b(xT, mixT, np.ascontiguousarray(p[1, 0].T), wd, True)
    return np.ascontiguousarray(outT.T).reshape(1, S, D).astype(np.float32)
```

```python
import math
import numpy as np
import ml_dtypes
from contextlib import ExitStack
import concourse.bass as bass
import concourse.mybir as mybir
from concourse.bass_utils import run_bass_kernel_spmd

F32 = mybir.dt.float32
BF16 = mybir.dt.bfloat16
AF = mybir.ActivationFunctionType
ALU = mybir.AluOpType
AX = mybir.AxisListType

D = 2048
KC = D // 128
DFF = 5632
FC = DFF // 128
NQ = 4
FQ = FC // NQ
PLE = 256
EPS = 1e-6
NCORES = 8


class Dep:
    __slots__ = ("w", "r")

    def __init__(self):
        self.w = None
        self.r = {}


class Eng:
    def __init__(self, key, eng, sem):
        self.key = key
        self.eng = eng
        self.sem = sem
        self.count = 0
        self.seen = {}


class KB:
    def __init__(self, nc, stack, n_dma_sems=48):
        self.nc = nc
        self.stack = stack
        self.sems = {}
        self.E = {}
        for key, eng in (("pe", nc.tensor), ("act", nc.scalar), ("dve", nc.vector),
                         ("pool", nc.gpsimd), ("sp", nc.sync)):
            sem = stack.enter_context(nc.semaphore("s_" + key))
            self.sems[key] = sem
            self.E[key] = Eng(key, eng, sem)
        self.dma_sems = []
        for i in range(n_dma_sems):
            k = "d%d" % i
            self.sems[k] = stack.enter_context(nc.semaphore("s_" + k))
            self.dma_sems.append([k, 0])
        self.dma_rr = 0
        self.n_inst = 0
        self._uid = 0

    def sb(self, shape, dtype, name=None):
        self._uid += 1
        return self.stack.enter_context(self.nc.sbuf_tensor("sb%d_%s" % (self._uid, name or ""), list(shape), dtype))

    def ps(self, shape, dtype, name=None):
        self._uid += 1
        return self.stack.enter_context(self.nc.psum_tensor("ps%d_%s" % (self._uid, name or ""), list(shape), dtype))

    def _wait(self, E, semkey, val):
        if val is None or val <= 0:
            return
        if E.seen.get(semkey, 0) >= val:
            return
        if semkey == E.key and E.key == "pe":
            return
        E.eng.wait_ge(self.sems[semkey], val)
        E.seen[semkey] = val

    def _deps(self, E, reads, writes):
        for d in reads:
            if d.w is not None:
                self._wait(E, d.w[0], d.w[1])
        for d in writes:
            if d.w is not None:
                self._wait(E, d.w[0], d.w[1])
            for sk, v in d.r.items():
                self._wait(E, sk, v)

    def _mark(self, ev, reads, writes):
        for d in reads:
            if d.r.get(ev[0], 0) < ev[1]:
                d.r[ev[0]] = ev[1]
        for d in writes:
            d.w = ev
            d.r = {}

    def op(self, ek, fn, reads=(), writes=(), inc=True):
        E = self.E[ek]
        self._deps(E, reads, writes)
        ins = fn(E.eng)
        inc = True
        if inc:
            E.count += 1
            ins.then_inc(E.sem, 1)
            ev = (E.key, E.count)
        else:
            ev = (E.key, E.count + 1)
        self._mark(ev, reads, writes)
        self.n_inst += 1
        return ev

    def dma(self, ek, out, in_, reads=(), writes=(), sw=None, **kw):
        E = self.E[ek]
        self._deps(E, reads, writes)
        if sw is not None:
            assert ek == "pool"
            if sw.gen > 0:
                self._wait(E, ("sw", sw.id, sw.gen), 16)
                E.eng.sem_clear(sw.sem)
            sw.gen += 1
            key = ("sw", sw.id, sw.gen)
            self.sems[key] = sw.sem
            ins = E.eng.dma_start(out=out, in_=in_, **kw)
            ins.then_inc(sw.sem, 16)
            ev = (key, 16)
            self._mark(ev, reads, writes)
            self.n_inst += 1
            return ev
        slot = self.dma_sems[self.dma_rr]
        self.dma_rr = (self.dma_rr + 1) % len(self.dma_sems)
        self._wait(E, slot[0], slot[1])
        ins = E.eng.dma_start(out=out, in_=in_, **kw)
        slot[1] += 16
        ins.then_inc(self.sems[slot[0]], 16)
        ev = (slot[0], slot[1])
        self._mark(ev, reads, writes)
        self.n_inst += 1
        return ev

    def barrier(self):
        for E in self.E.values():
            for E2 in self.E.values():
                if E2.count > 0:
                    if E2.key == E.key and E.key == "pe":
                        continue
                    self._wait(E, E2.key, E2.count)
            for k, v in self.dma_sems:
                self._wait(E, k, v)

    def finish(self, out_deps):
        E = self.E["sp"]
        for d in out_deps:
            if d.w is not None:
                self._wait(E, d.w[0], d.w[1])


class SwSlot:
    _n = 0

    def __init__(self, kb):
        SwSlot._n += 1
        self.id = SwSlot._n
        self.sem = kb.stack.enter_context(kb.nc.semaphore("s_sw%d" % self.id))
        self.gen = 0


class PsumPool:
    def __init__(self, kb, n=8):
        self.tiles = [kb.ps([128, 512], F32, name="psb%d" % i) for i in range(n)]
        self.deps = [Dep() for _ in range(n)]
        self.i = 0

    def get(self):
        t, d = self.tiles[self.i], self.deps[self.i]
        self.i = (self.i + 1) % len(self.tiles)
        return t, d

    def sub(self, idx):
        p = PsumPool.__new__(PsumPool)
        p.tiles = [self.tiles[i] for i in idx]
        p.deps = [self.deps[i] for i in idx]
        p.i = 0
        return p


class Ring:
    def __init__(self, kb, n, shape, dtype, name, sw=False):
        self.tiles = [kb.sb(shape, dtype, name="%s%d" % (name, i)) for i in range(n)]
        self.deps = [Dep() for _ in range(n)]
        self.sw = [SwSlot(kb) for _ in range(n)] if sw else None
        self.i = 0

    def get(self):
        t, d = self.tiles[self.i], self.deps[self.i]
        self.i = (self.i + 1) % len(self.tiles)
        return t, d

    def get3(self):
        i = self.i
        self.i = (self.i + 1) % len(self.tiles)
        return self.tiles[i], self.deps[i], self.sw[i]


def rstd_from_ss(kb, ss_ps, dss, out_sb, dout, n, inv_d=1.0 / D):
    kb.op("dve", lambda e: e.tensor_scalar(out_sb, ss_ps, inv_d, EPS, op0=ALU.mult, op1=ALU.add),
          reads=[dss], writes=[dout])
    kb.op("dve", lambda e: e.reciprocal(out_sb, out_sb), reads=[dout], writes=[dout])
    kb.op("act", lambda e: e.activation(out_sb, out_sb, AF.Sqrt), reads=[dout], writes=[dout])


class WLoader:
    def __init__(self, kb, nstage=2, stage_elems=2048):
        self.kb = kb
        self.stage = Ring(kb, nstage, [128, stage_elems], F32, "wst")
        self.k = 0

    def load(self, dst_ap2d, ddst, src_ap2d, n, q="sp"):
        kb = self.kb
        stg, dstg = self.stage.get()
        kb.dma(q, stg[:, 0:n], src_ap2d, writes=[dstg])
        eng = "pool"
        self.k += 1
        kb.op(eng, lambda e: e.tensor_copy(dst_ap2d, stg[:, 0:n]), reads=[dstg], writes=[ddst])


def build_phase_b(T, final):
    TG = 1024
    NG = T // TG
    NH = TG // 512
    nc = bass.Bass("TRN2", target_bir_lowering=False)
    xT_in = nc.dram_tensor("xT", [D, T], F32, kind="ExternalInput").ap()
    mixT = nc.dram_tensor("mixT", [D, T], BF16, kind="ExternalInput").ap()
    pT = nc.dram_tensor("pT", [PLE, T], F32, kind="ExternalInput").ap()
    w_out = nc.dram_tensor("w_out", [KC, 128, KC * 128], F32, kind="ExternalInput").ap()
    w_gate = nc.dram_tensor("w_gate", [FC, 128, KC * 128], F32, kind="ExternalInput").ap()
    w_up = nc.dram_tensor("w_up", [FC, 128, KC * 128], F32, kind="ExternalInput").ap()
    w_down = nc.dram_tensor("w_down", [NQ, KC, 128, FQ * 128], F32, kind="ExternalInput").ap()
    w_pg = nc.dram_tensor("w_pg", [KC, 128, KC * 128], F32, kind="ExternalInput").ap()
    w_pp = nc.dram_tensor("w_pp", [KC, 128, 2 * 128], F32, kind="ExternalInput").ap()
    g_ffn = nc.dram_tensor("g_ffn", [128, KC], F32, kind="ExternalInput").ap()
    g_fin = nc.dram_tensor("g_fin", [128, KC], F32, kind="ExternalInput").ap()
    outT = nc.dram_tensor("outT", [D, T], F32, kind="ExternalOutput").ap()

    with ExitStack() as st:
        kb = KB(nc, st)
        pp = PsumPool(kb)
        xres = kb.sb([128, KC, TG], F32, name="xres")
        dx = [Dep() for _ in range(KC)]
        actT = kb.sb([128, KC, TG], BF16, name="actT")
        da = [Dep() for _ in range(KC)]
        AT = kb.sb([128, FQ, TG], BF16, name="AT")
        dA = [Dep() for _ in range(FQ)]
        pTb = kb.sb([128, 2, TG], BF16, name="pTb")
        dp = Dep()
        gf = kb.sb([128, KC], F32, name="gf")
        dgf = Dep()
        gl = kb.sb([128, KC], F32, name="gl")
        dgl = Dep()
        ones = kb.sb([128, 128], BF16, name="ones")
        dones = Dep()
        rstd = kb.sb([128, TG], F32, name="rstd")
        drs = Dep()
        wl = WLoader(kb, nstage=3)
        wring = Ring(kb, 4, [128, KC * 128], BF16, "wr")
        wdring = Ring(kb, 2, [128, FQ * 128], BF16, "wd")
        wpring = Ring(kb, 2, [128, 2 * 128], BF16, "wp")
        sqring = Ring(kb, 2, [128, TG], BF16, "sq")
        tring = Ring(kb, 3, [128, 512], BF16, "tb")
        fring = Ring(kb, 2, [128, 512], F32, "tf")
        outdeps = []

        kb.dma("sp", gf[:], g_ffn, writes=[dgf])
        kb.dma("sp", gl[:], g_fin, writes=[dgl])
        kb.op("pool", lambda e: e.memset(ones[:], 1.0), writes=[dones])

        def rms_stats(src_deps):
            pss = [pp.get() for _ in range(NH)]
            for kc in range(KC):
                sq, dsq = sqring.get()
                kb.op("act", lambda e: e.activation(sq[:], xres[:, kc, :], AF.Square),
                      reads=[src_deps[kc]], writes=[dsq])
                for h, (pt, dpt) in enumerate(pss):
                    kb.op("pe", lambda e: e.matmul(pt[:], ones[:], sq[:, h * 512:(h + 1) * 512],
                                                   start=(kc == 0), stop=(kc == KC - 1)),
                          reads=[dsq, dones], writes=[dpt], inc=(kc == KC - 1))
            for h, (pt, dpt) in enumerate(pss):
                rstd_from_ss(kb, pt[:], dpt, rstd[:, h * 512:(h + 1) * 512], drs, 512)

        for g in range(NG):
            t0 = g * TG
            for kc in range(KC):
                kb.dma("sp", xres[:, kc, :], xT_in[kc * 128:(kc + 1) * 128, t0:t0 + TG], writes=[dx[kc]])
                kb.dma("act", actT[:, kc, :], mixT[kc * 128:(kc + 1) * 128, t0:t0 + TG], writes=[da[kc]])
            for j in range(2):
                wl.load(pTb[:, j, 0:512], dp, pT[j * 128:(j + 1) * 128, t0:t0 + 512], 512)
                wl.load(pTb[:, j, 512:1024], dp, pT[j * 128:(j + 1) * 128, t0 + 512:t0 + 1024], 512)
            for cb in range(KC):
                w, dw = wring.get()
                wl.load(w[:], dw, w_out[cb], KC * 128)
                for h in range(NH):
                    sl = slice(h * 512, (h + 1) * 512)
                    pt, dpt = pp.get()
                    for kc in range(KC):
                        kb.op("pe", lambda e: e.matmul(pt[:], w[:, kc * 128:(kc + 1) * 128], actT[:, kc, sl],
                                                       start=(kc == 0), stop=(kc == KC - 1)),
                              reads=[dw, da[kc]], writes=[dpt])
                    kb.op("dve", lambda e: e.tensor_tensor(xres[:, cb, sl], xres[:, cb, sl], pt[:], op=ALU.add),
                          reads=[dpt, dx[cb]], writes=[dx[cb]])
            rms_stats(dx)
            for kc in range(KC):
                kb.op("dve", lambda e: e.scalar_tensor_tensor(actT[:, kc, :], xres[:, kc, :], gf[:, kc:kc + 1],
                                                              rstd[:], op0=ALU.mult, op1=ALU.mult),
                      reads=[dx[kc], dgf, drs], writes=[da[kc]])
            for fq in range(NQ):
                for fc in range(FQ):
                    f = fq * FQ + fc
                    wg, dwg = wring.get()
                    wl.load(wg[:], dwg, w_gate[f], KC * 128)
                    wu, dwu = wring.get()
                    wl.load(wu[:], dwu, w_up[f], KC * 128)
                    for h in range(NH):
                        sl = slice(h * 512, (h + 1) * 512)
                        pg, dpg = pp.get()
                        pu, dpu = pp.get()
                        for kc in range(KC):
                            kb.op("pe", lambda e: e.matmul(pg[:], wg[:, kc * 128:(kc + 1) * 128], actT[:, kc, sl],
                                                           start=(kc == 0), stop=(kc == KC - 1)),
                                  reads=[dwg, da[kc]], writes=[dpg])
                        for kc in range(KC):
                            kb.op("pe", lambda e: e.matmul(pu[:], wu[:, kc * 128:(kc + 1) * 128], actT[:, kc, sl],
                                                           start=(kc == 0), stop=(kc == KC - 1)),
                                  reads=[dwu, da[kc]], writes=[dpu])
                        tb, dtb = tring.get()
                        kb.op("act", lambda e: e.activation(tb[:], pg[:], AF.Silu), reads=[dpg], writes=[dtb])
                        kb.op("dve", lambda e: e.tensor_tensor(AT[:, fc, sl], tb[:], pu[:], op=ALU.mult),
                              reads=[dtb, dpu], writes=[dA[fc]])
                for cb in range(KC):
                    wd, dwd = wdring.get()
                    wl.load(wd[:], dwd, w_down[fq, cb], FQ * 128)
                    for h in range(NH):
                        sl = slice(h * 512, (h + 1) * 512)
                        pt, dpt = pp.get()
                        for fc in range(FQ):
                            kb.op("pe", lambda e: e.matmul(pt[:], wd[:, fc * 128:(fc + 1) * 128], AT[:, fc, sl],
                                                           start=(fc == 0), stop=(fc == FQ - 1)),
                                  reads=[dwd, dA[fc]], writes=[dpt])
                        kb.op("dve", lambda e: e.tensor_tensor(xres[:, cb, sl], xres[:, cb, sl], pt[:], op=ALU.add),
                              reads=[dpt, dx[cb]], writes=[dx[cb]])
            for kc in range(KC):
                kb.op("act", lambda e: e.activation(actT[:, kc, :], xres[:, kc, :], AF.Copy),
                      reads=[dx[kc]], writes=[da[kc]])
            for cb in range(KC):
                w, dw = wring.get()
                wl.load(w[:], dw, w_pg[cb], KC * 128)
                wq, dwq = wpring.get()
                wl.load(wq[:], dwq, w_pp[cb], 256)
                for h in range(NH):
                    sl = slice(h * 512, (h + 1) * 512)
                    pg, dpg = pp.get()
                    pq, dpq = pp.get()
                    for kc in range(KC):
                        kb.op("pe", lambda e: e.matmul(pg[:], w[:, kc * 128:(kc + 1) * 128], actT[:, kc, sl],
                                                       start=(kc == 0), stop=(kc == KC - 1)),
                              reads=[dw, da[kc]], writes=[dpg])
                    for j in range(2):
                        kb.op("pe", lambda e: e.matmul(pq[:], wq[:, j * 128:(j + 1) * 128], pTb[:, j, sl],
                                                       start=(j == 0), stop=(j == 1)),
                              reads=[dwq, dp], writes=[dpq], inc=(j == 1))
                    tf, dtf = fring.get()
                    kb.op("act", lambda e: e.activation(tf[:], pg[:], AF.Sigmoid), reads=[dpg], writes=[dtf])
                    kb.op("dve", lambda e: e.tensor_tensor(tf[:], tf[:], pq[:], op=ALU.mult),
                          reads=[dtf, dpq], writes=[dtf])
                    kb.op("dve", lambda e: e.tensor_tensor(xres[:, cb, sl], xres[:, cb, sl], tf[:], op=ALU.add),
                          reads=[dtf, dx[cb]], writes=[dx[cb]])
            if final:
                rms_stats(dx)
                for kc in range(KC):
                    kb.op("dve", lambda e: e.scalar_tensor_tensor(xres[:, kc, :], xres[:, kc, :], gl[:, kc:kc + 1],
                                                                  rstd[:], op0=ALU.mult, op1=ALU.mult),
                          reads=[dx[kc], dgl, drs], writes=[dx[kc]])
            for kc in range(KC):
                do = Dep()
                kb.dma("sp", outT[kc * 128:(kc + 1) * 128, t0:t0 + TG], xres[:, kc, :], reads=[dx[kc]], writes=[do])
                outdeps.append(do)
        kb.finish(outdeps)
    return nc


def prep_phase_b_weights(w_out, ffn_gate, ffn_up, ffn_down, ple_gate, ple_proj, norm_ffn, final_norm):
    def blk(w, kcn):
        K, C = w.shape
        return np.ascontiguousarray(w.reshape(kcn, 128, C // 128, 128).transpose(2, 1, 0, 3)).reshape(C // 128, 128, kcn * 128)
    d = {}
    d["w_out"] = blk(w_out, KC)
    d["w_gate"] = blk(ffn_gate, KC)
    d["w_up"] = blk(ffn_up, KC)
    d["w_down"] = np.ascontiguousarray(
        ffn_down.reshape(NQ, FQ, 128, KC, 128).transpose(0, 3, 2, 1, 4)).reshape(NQ, KC, 128, FQ * 128)
    d["w_pg"] = blk(ple_gate, KC)
    d["w_pp"] = blk(ple_proj, 2)
    d["g_ffn"] = np.ascontiguousarray(norm_ffn.reshape(KC, 128).T)
    d["g_fin"] = np.ascontiguousarray(final_norm.reshape(KC, 128).T)
    return d


def load_weights_scaled(kb, wl, dst, ddst, src, gcol, dg, ncols):
    per = max(1, 2048 // ncols)
    kc = 0
    while kc < KC:
        n = min(per, KC - kc)
        stg, dstg = wl.stage.get()
        kb.dma("sp", stg[:, 0:n * ncols].rearrange("p (a b) -> p a b", a=n), src[:, kc:kc + n, :], writes=[dstg])
        for a in range(n):
            kb.op("dve", lambda e: e.tensor_scalar(dst[:, kc + a, :], stg[:, a * ncols:(a + 1) * ncols],
                                                   gcol[:, kc + a:kc + a + 1], None, op0=ALU.mult),
                  reads=[dstg, dg], writes=[ddst])
        kc += n


def projection_pass(kb, pp, S, xT, wf, dwf, NF, wt, dwt, NTM, f_out, t_out, ones, dones, fl_cb=None,
                    xring=None, hring=None, sqring=None):
    NG = S // 512
    fdeps = [[Dep() for _ in range(NG)] for _ in range(NF)]
    rsb = kb.sb([128, 512], F32, name="rs_bc")
    drsb = Dep()
    rsc = kb.sb([128, 4], F32, name="rs_col")
    drsc = Dep()
    evr = Ring(kb, 2 * (NF + 4) + 2, [128, 512], BF16, "pev")
    pend = []
    e0 = kb.sb([128, 1], F32, name="e0")
    de0 = Dep()
    kb.op("pool", lambda e: e.memset(e0[:], 0.0), writes=[de0])
    kb.op("pool", lambda e: e.memset(e0[0:1, :], 1.0), writes=[de0])
    tmpc = Ring(kb, 2, [128, 1], F32, "ptc")
    for g in range(NG):
        t0 = g * 512
        hT, dh = hring.get()
        pss, dpss = pp.get()
        psc, dpsc = pp.get()
        for kc in range(KC):
            xt, dxt = xring.get()
            kb.dma("sp", xt[:], xT[kc * 128:(kc + 1) * 128, t0:t0 + 512], writes=[dxt])
            kb.op("dve", lambda e: e.tensor_copy(hT[:, kc, :], xt[:]), reads=[dxt], writes=[dh])
            sq, dsq = sqring.get()
            kb.op("act", lambda e: e.activation(sq[:], xt[:], AF.Square), reads=[dxt], writes=[dsq])
            kb.op("pe", lambda e: e.matmul(pss[:], ones[:], sq[:], start=(kc == 0), stop=(kc == KC - 1)),
                  reads=[dsq, dones], writes=[dpss])
        for (o_, i_, r_, w_) in pend:
            kb.dma("act", o_, i_, reads=r_, writes=w_)
        pend.clear()
        rstd_from_ss(kb, pss[:], dpss, rsb[:], drsb, 512)
        for gi in range(NF):
            pt, dpt = pp.get()
            for kc in range(KC):
                kb.op("pe", lambda e: e.matmul(pt[:], wf[:, gi, kc, :], hT[:, kc, :],
                                               start=(kc == 0), stop=(kc == KC - 1)),
                      reads=[dwf, dh], writes=[dpt], inc=(kc == KC - 1))
            ev, dev = evr.get()
            kb.op("dve", lambda e: e.tensor_tensor(ev[:], pt[:], rsb[:], op=ALU.mult),
                  reads=[dpt, drsb], writes=[dev])
            pend.append((f_out[gi][:, t0:t0 + 512], ev[:], [dev], [fdeps[gi][g]]))
        for tt in range(4):
            kb.op("pe", lambda e: e.matmul(psc[:, tt:tt + 1], rsb[:, tt * 128:(tt + 1) * 128], e0[:, 0:1],
                                           start=True, stop=True),
                  reads=[drsb, de0], writes=[dpsc])
        kb.op("dve", lambda e: e.tensor_copy(rsc[:], psc[:, 0:4]), reads=[dpsc], writes=[drsc])
        for tt in range(4):
            tok0 = t0 + tt * 128
            if NTM > 0:
                pt, dpt = pp.get()
                for kc in range(KC):
                    kb.op("pe", lambda e: e.matmul(pt[:, 0:NTM], hT[:, kc, tt * 128:(tt + 1) * 128], wt[:, kc, 0:NTM],
                                                   start=(kc == 0), stop=(kc == KC - 1)),
                          reads=[dwt, dh], writes=[dpt], inc=(kc == KC - 1))
                ev, dev = evr.get()
                for (c0, n, dram, deps, sc, dsc) in t_out:
                    if sc is None:
                        kb.op("act", lambda e: e.activation(ev[:, c0:c0 + n], pt[:, c0:c0 + n], AF.Copy,
                                                            scale=rsc[:, tt:tt + 1]),
                              reads=[dpt, drsc], writes=[dev])
                    else:
                        tc_, dtc = tmpc.get()
                        kb.op("dve", lambda e: e.tensor_tensor(tc_[:], rsc[:, tt:tt + 1], sc, op=ALU.mult),
                              reads=[drsc, dsc], writes=[dtc])
                        kb.op("act", lambda e: e.activation(ev[:, c0:c0 + n], pt[:, c0:c0 + n], AF.Copy,
                                                            scale=tc_[:]),
                              reads=[dpt, dtc], writes=[dev])
                for (c0, n, dram, deps, sc, dsc) in t_out:
                    pend.append((dram[tok0:tok0 + 128, :], ev[:, c0:c0 + n], [dev], [deps[g * 4 + tt]]))
            if fl_cb is not None:
                pf, dpf = pp.get()
                for kc in range(KC):
                    kb.op("pe", lambda e: e.matmul(pf[:, 0:1], hT[:, kc, tt * 128:(tt + 1) * 128], wt[:, kc, NTM:NTM + 1],
                                                   start=(kc == 0), stop=(kc == KC - 1)),
                          reads=[dwt, dh], writes=[dpf], inc=(kc == KC - 1))
                fl_cb(g * 4 + tt, pf, dpf, rsc[:, tt:tt + 1], drsc)
    for (o_, i_, r_, w_) in pend:
        kb.dma("act", o_, i_, reads=r_, writes=w_)
    pend.clear()
    return fdeps


def attention_pass(kb, pp, S, qT_dram, dq_deps, krow0, krows, kT, dkT, v, dv, scale, bias_fn, mult_fn, ktiles_fn,
                   out_fn, ones, dones, extra_fn=None, pre_fn=None, name="at", look=1):
    NQT = S // 512
    GSZ = 2
    pacc = pp.sub([0, 1])
    pl = pp.sub([2, 3])
    pst = pp.sub([4, 5, 6, 7])
    qring = Ring(kb, 3, [128, 512], BF16, name + "q")
    if krows < 128:
        for t_, d_ in zip(qring.tiles, qring.deps):
            kb.op("pool", lambda e: e.memset(t_[:], 0.0), writes=[d_])
    pring = Ring(kb, 8, [128, 512], BF16, name + "p")
    mring = Ring(kb, 8, [128, 512], BF16, name + "m")
    lring = Ring(kb, 3, [128, 512], F32, name + "l")
    onesf = kb.sb([128, 128], F32, name=name + "1f")
    donf = Dep()
    kb.op("pool", lambda e: e.memset(onesf[:], 1.0), writes=[donf])
    items = []
    for j in range(NQT):
        kts = ktiles_fn(j)
        for n_i, i in enumerate(kts):
            items.append((j, n_i, i, len(kts)))
    qstate = {}
    sstate = {}

    def issue_s(t, inc=True):
        j, n_i, i, nk = items[t]
        if n_i == 0:
            qt, dqt = qring.get()
            kb.dma("sp", qt[krow0:krow0 + krows, :], qT_dram[krow0:krow0 + krows, j * 512:(j + 1) * 512],
                   reads=[dq_deps[j]], writes=[dqt])
            bias_ap, dbias = bias_fn(j)
            if pre_fn is not None:
                pre_fn(j)
            qstate[j] = dict(qt=qt, dqt=dqt, bias=bias_ap, dbias=dbias)
        q = qstate[j]
        st, dst = pst.get()
        ex = extra_fn(i, j) if extra_fn is not None else None
        kb.op("pe", lambda e: e.matmul(st[:], kT[:, i * 128:(i + 1) * 128],
                                       q["qt"][:, :], start=True, stop=(ex is None)),
              reads=[dkT, q["dqt"]], writes=[dst], inc=(inc and ex is None))
        if ex is not None:
            lhsT_ap, rhs_ap, exdeps = ex
            kb.op("pe", lambda e: e.matmul(st[:], lhsT_ap, rhs_ap, start=False, stop=True),
                  reads=exdeps, writes=[dst], inc=inc)
        pt_, dpt_ = pring.get()
        b = q["bias"](i) if callable(q["bias"]) else q["bias"]
        kb.op("act", lambda e: e.activation(pt_[:], st[:], AF.Exp, bias=b, scale=scale),
              reads=[dst, q["dbias"]], writes=[dpt_])
        m = mult_fn(i, j)
        if m is not None:
            m_ap, dm = m
            pm, dpm = mring.get()
            kb.op("dve", lambda e: e.tensor_tensor(pm[:], pt_[:], m_ap, op=ALU.mult),
                  reads=[dpt_, dm], writes=[dpm])
            pt_, dpt_ = pm, dpm
        pe_l = (n_i % 2 == 1)
        if n_i == 0:
            la, dla = lring.get()
            q["la"], q["dla"] = la, dla
            kb.op("dve", lambda e: e.tensor_copy(la[:], pt_[:]), reads=[dpt_], writes=[dla])
        elif not pe_l:
            la, dla = q["la"], q["dla"]
            kb.op("dve", lambda e: e.tensor_tensor(la[:], la[:], pt_[:], op=ALU.add), reads=[dpt_, dla], writes=[dla])
        sstate[t] = (pt_, dpt_, pe_l)

    def issue_pv(t, inc=True):
        j, n_i, i, nk = items[t]
        q = qstate[j]
        pt_, dpt_, pe_l = sstate.pop(t)
        if n_i == 0:
            q["O"], q["dO"] = pacc.get()
            q["L"], q["dL"] = pl.get()
            q["lstart"] = True
        first, last = (n_i == 0), (n_i == nk - 1)
        kb.op("pe", lambda e: e.matmul(q["O"][:], v[:, i, :], pt_[:], start=first, stop=last),
              reads=[dv, dpt_], writes=[q["dO"]])
        if pe_l:
            ls = q["lstart"]
            kb.op("pe", lambda e: e.matmul(q["L"][:], ones[:], pt_[:], start=ls, stop=False),
                  reads=[dones, dpt_], writes=[q["dL"]])
            q["lstart"] = False
        if last:
            ls = q["lstart"]
            kb.op("pe", lambda e: e.matmul(q["L"][:], onesf[:], q["la"][:], start=ls, stop=True),
                  reads=[donf, q["dla"]], writes=[q["dL"]])
            out_fn(j, q["O"], q["dO"], q["L"], q["dL"])
            del qstate[j]

    n = len(items)
    ngr = (n + GSZ - 1) // GSZ
    for g in range(ngr + look):
        if g < ngr:
            hi = min(n, (g + 1) * GSZ)
            for t in range(g * GSZ, hi):
                issue_s(t, inc=(t == hi - 1))
        gp = g - look
        if gp >= 0:
            ts = list(range(gp * GSZ, min(n, (gp + 1) * GSZ)))
            for t in reversed(ts):
                d = sstate[t][1]
                if d.w is not None:
                    kb._wait(kb.E["pe"], d.w[0], d.w[1])
            for t in ts:
                issue_pv(t, inc=(t == ts[-1]))


def build_phase_a0(S):
    NT = S // 128
    NG = S // 512
    nc = bass.Bass("TRN2", target_bir_lowering=False)
    xT = nc.dram_tensor("xT", [D, S], F32, kind="ExternalInput").ap()
    wf_d = nc.dram_tensor("wf", [128, 5, KC, 128], F32, kind="ExternalInput").ap()
    wt_d = nc.dram_tensor("wt", [128, KC, 385], F32, kind="ExternalInput").ap()
    gmix_d = nc.dram_tensor("gmix", [128, KC], F32, kind="ExternalInput").ap()
    fb_d = nc.dram_tensor("fb", [128, 1], F32, kind="ExternalInput").ap()
    gn_d = nc.dram_tensor("gn", [128, 1], F32, kind="ExternalInput").ap()
    kw_d = nc.dram_tensor("kw", [128, 1], F32, kind="ExternalInput").ap()
    cd_d = nc.dram_tensor("cd", [128, 1], F32, kind="ExternalInput").ap()
    qw_d = nc.dram_tensor("qw", [128, 512], F32, kind="ExternalInput").ap()
    dmat_d = nc.dram_tensor("dmat", [128, 512], F32, kind="ExternalInput").ap()
    mu_d = nc.dram_tensor("mu", [128, 4, 512], BF16, kind="ExternalInput").ap()
    tri_d = nc.dram_tensor("tri", [128, 3, 128], F32, kind="ExternalInput").ap()
    mixT = nc.dram_tensor("mixT", [256, S], BF16, kind="ExternalOutput").ap()
    names = ["rqT", "rkT", "rgT", "fqT", "fkT"]
    fT = [nc.dram_tensor("s_" + n, [128, S], BF16, kind="Internal").ap() for n in names]
    rkw_d = nc.dram_tensor("s_rkw", [S, 128], BF16, kind="Internal").ap()
    rv_d = nc.dram_tensor("s_rv", [S, 128], BF16, kind="Internal").ap()
    fv_d = nc.dram_tensor("s_fv", [S, 128], BF16, kind="Internal").ap()

    with ExitStack() as st:
        kb = KB(nc, st)
        pp = PsumPool(kb)
        outdeps = []
        ones = kb.sb([128, 128], BF16, name="ones")
        dones = Dep()
        kb.op("pool", lambda e: e.memset(ones[:], 1.0), writes=[dones])

        def small(name, shape, dt, src):
            t = kb.sb(shape, dt, name=name)
            d = Dep()
            kb.dma("sp", t[:], src, writes=[d])
            return t, d
        gmix, dgm = small("gmix", [128, KC], F32, gmix_d)
        fb, dfb = small("fb", [128, 1], F32, fb_d)
        gn, dgn = small("gn", [128, 1], F32, gn_d)
        kw, dkw = small("kw", [128, 1], F32, kw_d)
        cd, dcd = small("cd", [128, 1], F32, cd_d)
        qw, dqw = small("qw", [128, 512], F32, qw_d)
        dmat, ddm = small("dmat", [128, 512], F32, dmat_d)
        mu, dmu = small("mu", [128, 4, 512], BF16, mu_d)
        tri, dtri = small("tri", [128, 3, 128], F32, tri_d)

        zf = kb.sb([128, NT], F32, name="zf")
        dzf = Dep()
        lf = kb.sb([128, NT], F32, name="lf")
        dlf = Dep()
        cs = kb.sb([128, NT], F32, name="cs")
        dcs = Dep()
        cbc = kb.sb([128, NT], F32, name="cbc")
        dcbc = Dep()
        totT = kb.sb([128, 128], F32, name="totT")
        dtot = Dep()
        onesf = kb.sb([128, 128], F32, name="onesf")
        donf = Dep()
        with ExitStack() as st1:
            kb.stack = st1
            wl = WLoader(kb, nstage=2)
            wf = kb.sb([128, 5, KC, 128], BF16, name="wf")
            dwf = Dep()
            wt = kb.sb([128, KC, 385], BF16, name="wt")
            dwt = Dep()
            for gi in range(5):
                load_weights_scaled(kb, wl, wf[:, gi], dwf, wf_d[:, gi], gmix, dgm, 128)
            load_weights_scaled(kb, wl, wt, dwt, wt_d, gmix, dgm, 385)
            xring = Ring(kb, 8, [128, 512], F32, "xr")
            hring = Ring(kb, 2, [128, KC, 512], BF16, "hr")
            sqring = Ring(kb, 16, [128, 512], BF16, "sqr")
            tdeps = {n: [Dep() for _ in range(NT)] for n in ("rkw", "rv", "fv")}

            def fl_cb(t, pf, dpf, rcol, drc):
                kb.op("dve", lambda e: e.tensor_scalar(zf[:, t:t + 1], pf[:, 0:1], rcol, fb[:], op0=ALU.mult, op1=ALU.add),
                      reads=[dpf, drc, dfb], writes=[dzf])
            t_out = [(0, 128, rkw_d, tdeps["rkw"], kw[:], dkw), (128, 128, rv_d, tdeps["rv"], None, None),
                     (256, 128, fv_d, tdeps["fv"], None, None)]
            fdeps = projection_pass(kb, pp, S, xT, wf, dwf, 5, wt, dwt, 384, fT, t_out, ones, dones, fl_cb=fl_cb,
                                    xring=xring, hring=hring, sqring=sqring)
            kb.op("act", lambda e: e.activation(lf[:], zf[:], AF.Exp, scale=-1.0), reads=[dzf], writes=[dlf])
            kb.op("act", lambda e: e.activation(lf[:], lf[:], AF.Ln, bias=1.0), reads=[dlf], writes=[dlf])
            kb.op("dve", lambda e: e.tensor_scalar(lf[:], lf[:], -1.0, None, op0=ALU.mult), reads=[dlf], writes=[dlf])
            kb.op("pool", lambda e: e.memset(onesf[:], 1.0), writes=[donf])
            p1, dp1 = pp.get()
            kb.op("pe", lambda e: e.matmul(p1[0:NT, 0:128], lf[:, 0:NT], onesf[:], start=True, stop=True),
                  reads=[dlf, donf], writes=[dp1])
            kb.op("dve", lambda e: e.memset(totT[:], 0.0), writes=[dtot])
            kb.op("dve", lambda e: e.tensor_copy(totT[0:NT, :], p1[0:NT, 0:128]), reads=[dp1], writes=[dtot])
            p2, dp2 = pp.get()
            kb.op("pe", lambda e: e.matmul(p2[:, 0:NT], tri[:, 0, :], lf[:, 0:NT], start=True, stop=False),
                  reads=[dlf, dtri], writes=[dp2])
            kb.op("pe", lambda e: e.matmul(p2[:, 0:NT], totT[:], tri[:, 1, 0:NT], start=False, stop=True),
                  reads=[dtot, dtri], writes=[dp2])
            kb.op("dve", lambda e: e.tensor_copy(cs[:], p2[:, 0:NT]), reads=[dp2], writes=[dcs])
            p3, dp3 = pp.get()
            kb.op("pe", lambda e: e.matmul(p3[:, 0:NT], tri[:, 2, :], cs[:], start=True, stop=True),
                  reads=[dcs, dtri], writes=[dp3])
            kb.op("dve", lambda e: e.tensor_copy(cbc[:], p3[:, 0:NT]), reads=[dp3], writes=[dcbc])
            kb.barrier()
        kb.stack = st

        with ExitStack() as st2:
            kb.stack = st2
            state = kb.sb([128, 128], F32, name="state")
            dst_ = Dep()
            kb.op("dve", lambda e: e.memset(state[:], 0.0), writes=[dst_])
            prevb = Ring(kb, 8, [128, 128], BF16, "prev")
            inr = {n: Ring(kb, 2, [128, 512], BF16, "ri_" + n) for n in ("q", "k", "g", "kw", "v")}
            qwr = Ring(kb, 2, [128, 512], BF16, "qwr")
            attr = Ring(kb, 2, [128, 512], BF16, "attr")
            f32r = Ring(kb, 6, [128, 512], F32, "f32r")
            b16r = Ring(kb, 4, [128, 512], BF16, "b16r")
            onesd = kb.sb([128, 128], BF16, name="onesd")
            dond = Dep()
            kb.op("pool", lambda e: e.memset(onesd[:], 1.0 / 128.0), writes=[dond])
            for g in range(NG):
                t0 = g * 512
                tl = {}
                for n, src, dl in (("q", fT[0], fdeps[0][g]), ("k", fT[1], fdeps[1][g]), ("g", fT[2], fdeps[2][g])):
                    t_, d_ = inr[n].get()
                    kb.dma("sp", t_[:], src[:, t0:t0 + 512], reads=[dl], writes=[d_])
                    tl[n] = (t_, d_)
                for n, src, dl in (("kw", rkw_d, tdeps["rkw"]), ("v", rv_d, tdeps["rv"])):
                    t_, d_ = inr[n].get()
                    kb.dma("act", t_[:].rearrange("p (a b) -> p a b", a=4),
                           src[t0:t0 + 512, :].rearrange("(a p) b -> p a b", p=128),
                           reads=dl[g * 4:g * 4 + 4], writes=[d_])
                    tl[n] = (t_, d_)
                q_, dq_ = tl["q"]
                k_, dk_ = tl["k"]
                g_, dg_ = tl["g"]
                kw_, dkw_ = tl["kw"]
                v_, dv_ = tl["v"]
                pu, dpu = pp.get()
                for n in range(4):
                    sl = slice(n * 128, (n + 1) * 128)
                    kb.op("pe", lambda e: e.matmul(pu[:, sl], kw_[:, sl], v_[:, sl], start=True, stop=True),
                          reads=[dkw_, dv_], writes=[dpu])
                prevs = []
                for n in range(4):
                    sl = slice(n * 128, (n + 1) * 128)
                    pb, dpb = prevb.get()
                    kb.op("dve", lambda e: e.tensor_copy(pb[:], state[:]), reads=[dst_], writes=[dpb])
                    kb.op("dve", lambda e: e.scalar_tensor_tensor(state[:], state[:], cd[:], pu[:, sl],
                                                                  op0=ALU.mult, op1=ALU.add),
                          reads=[dst_, dcd, dpu], writes=[dst_])
                    prevs.append((pb, dpb))
                pa, dpa = pp.get()
                for n in range(4):
                    sl = slice(n * 128, (n + 1) * 128)
                    kb.op("pe", lambda e: e.matmul(pa[:, sl], k_[:, sl], q_[:, sl], start=True, stop=True),
                          reads=[dk_, dq_], writes=[dpa])
                at_, dat_ = attr.get()
                kb.op("dve", lambda e: e.tensor_tensor(at_[:], pa[:], dmat[:], op=ALU.mult),
                      reads=[dpa, ddm], writes=[dat_])
                qw_, dqw_ = qwr.get()
                kb.op("pool", lambda e: e.tensor_tensor(qw_[:], q_[:], qw[:], op=ALU.mult),
                      reads=[dq_, dqw], writes=[dqw_])
                py, dpy = pp.get()
                for n in range(4):
                    sl = slice(n * 128, (n + 1) * 128)
                    kb.op("pe", lambda e: e.matmul(py[:, sl], v_[:, sl], at_[:, sl], start=True, stop=False),
                          reads=[dv_, dat_], writes=[dpy])
                    kb.op("pe", lambda e: e.matmul(py[:, sl], prevs[n][0][:], qw_[:, sl], start=False, stop=True),
                          reads=[prevs[n][1], dqw_], writes=[dpy])
                yb, dyb = b16r.get()
                kb.op("act", lambda e: e.activation(yb[:], py[:], AF.Copy), reads=[dpy], writes=[dyb])
                y2, dy2 = b16r.get()
                kb.op("act", lambda e: e.activation(y2[:], py[:], AF.Square), reads=[dpy], writes=[dy2])
                pm, dpm = pp.get()
                kb.op("pe", lambda e: e.matmul(pm[:], onesd[:], yb[:], start=True, stop=True),
                      reads=[dond, dyb], writes=[dpm])
                pe2, dpe2 = pp.get()
                kb.op("pe", lambda e: e.matmul(pe2[:], onesd[:], y2[:], start=True, stop=True),
                      reads=[dond, dy2], writes=[dpe2])
                mean, dmean = f32r.get()
                kb.op("act", lambda e: e.activation(mean[:], pm[:], AF.Copy), reads=[dpm], writes=[dmean])
                var, dvar = f32r.get()
                kb.op("dve", lambda e: e.tensor_tensor(var[:], mean[:], mean[:], op=ALU.mult),
                      reads=[dmean], writes=[dvar])
                kb.op("dve", lambda e: e.tensor_tensor(var[:], pe2[:], var[:], op=ALU.subtract),
                      reads=[dpe2, dvar], writes=[dvar])
                kb.op("dve", lambda e: e.tensor_scalar(var[:], var[:], EPS, None, op0=ALU.add), reads=[dvar], writes=[dvar])
                kb.op("dve", lambda e: e.reciprocal(var[:], var[:]), reads=[dvar], writes=[dvar])
                kb.op("act", lambda e: e.activation(var[:], var[:], AF.Sqrt), reads=[dvar], writes=[dvar])
                yc, dyc = f32r.get()
                kb.op("dve", lambda e: e.tensor_tensor(yc[:], py[:], mean[:], op=ALU.subtract),
                      reads=[dpy, dmean], writes=[dyc])
                kb.op("dve", lambda e: e.tensor_tensor(yc[:], yc[:], var[:], op=ALU.mult),
                      reads=[dyc, dvar], writes=[dyc])
                sg, dsg = f32r.get()
                kb.op("act", lambda e: e.activation(sg[:], g_[:], AF.Silu), reads=[dg_], writes=[dsg])
                ob, dob = b16r.get()
                kb.op("dve", lambda e: e.scalar_tensor_tensor(ob[:], yc[:], gn[:], sg[:], op0=ALU.mult, op1=ALU.mult),
                      reads=[dyc, dgn, dsg], writes=[dob])
                do = Dep()
                kb.dma("sp", mixT[0:128, t0:t0 + 512], ob[:], reads=[dob], writes=[do])
                outdeps.append(do)
            kb.barrier()
        kb.stack = st

        with ExitStack() as st3:
            kb.stack = st3
            kT = kb.sb([128, S], BF16, name="kT")
            dkT = Dep()
            vv = kb.sb([128, NT, 128], BF16, name="vv")
            dvv = Dep()
            for g in range(NG):
                kb.dma("sp", kT[:, g * 512:(g + 1) * 512], fT[4][:, g * 512:(g + 1) * 512], reads=[fdeps[4][g]], writes=[dkT])
                kb.dma("act", vv[:, g * 4:(g + 1) * 4, :],
                       fv_d[g * 512:(g + 1) * 512, :].rearrange("(a p) b -> p a b", p=128),
                       reads=tdeps["fv"][g * 4:g * 4 + 4], writes=[dvv])
            bring = Ring(kb, 2, [128, NT], F32, "fbias")
            orr = Ring(kb, 2, [128, 512], F32, "fo_r")
            obr = Ring(kb, 2, [128, 512], BF16, "fo_b")

            def bias_fn(j):
                bt, dbt = bring.get()
                col = min(4 * j + 1, NT - 1)
                kb.op("dve", lambda e: e.tensor_scalar(bt[:], cs[:], -1.0, cbc[:, col:col + 1], op0=ALU.mult, op1=ALU.add),
                      reads=[dcs, dcbc], writes=[dbt])
                return (lambda i: bt[:, i:i + 1]), dbt

            def mult_fn(i, j):
                u = i - 4 * j
                if u >= 0:
                    return mu[:, u, :], dmu
                return None

            def out_fn(j, Ops, dO, Lps, dL):
                r, dr = orr.get()
                kb.op("dve", lambda e: e.reciprocal(r[:], Lps[:]), reads=[dL], writes=[dr])
                ob, dob = obr.get()
                kb.op("dve", lambda e: e.tensor_tensor(ob[:], Ops[:], r[:], op=ALU.mult), reads=[dO, dr], writes=[dob])
                do = Dep()
                kb.dma("sp", mixT[128:256, j * 512:(j + 1) * 512], ob[:], reads=[dob], writes=[do])
                outdeps.append(do)

            attention_pass(kb, pp, S, fT[3], fdeps[3], 0, 128, kT, dkT, vv, dvv, 128 ** -0.5, bias_fn, mult_fn,
                           lambda j: list(range(0, 4 * j + 4)), out_fn, ones, dones, name="fx")
        kb.stack = st
        kb.finish(outdeps)
    return nc


def ret_consts(h):
    lg = math.log1p(-2.0 ** (-5.0 - h))
    p = np.arange(128, dtype=np.float64)
    sc = 128 ** -0.5
    kw = (np.exp(lg * (127 - p)) * sc).astype(np.float32).reshape(128, 1)
    cdv = np.full((128, 1), math.exp(lg * 128), np.float32)
    qw = np.tile(np.exp(lg * (p + 1.0))[None, :], (128, 4)).astype(np.float32)
    rel = p[None, :] - p[:, None]
    dm = np.where(rel >= 0, np.exp(lg * np.maximum(rel, 0)), 0.0) * sc
    dmat = np.tile(dm, (1, 4)).astype(np.float32)
    return kw, cdv, qw, dmat


def causal_mu():
    s = np.arange(128)[:, None]
    q = np.arange(512)[None, :]
    out = np.zeros((128, 4, 512), np.float32)
    for u in range(4):
        delta = q // 128 - u
        out[:, u, :] = np.where(delta > 0, 1.0, np.where(delta == 0, ((q % 128) >= s) * 1.0, 0.0))
    return out.astype(ml_dtypes.bfloat16)


def tri_consts():
    p = np.arange(128)
    U = (p[:, None] <= p[None, :]).astype(np.float32)
    SU = (p[:, None] < p[None, :]).astype(np.float32)
    Sel = np.zeros((128, 128), np.float32)
    Sel[127, :] = 1.0
    return np.ascontiguousarray(np.stack([U, SU, Sel], axis=1))


def prep_phase_a0(c, w_in_even, norm_mix0, ret_gn, fox_fb):
    RW = 1024
    cols = lambda base: slice(base + c * 128, base + (c + 1) * 128)
    w = w_in_even
    rq, rk, rv, rg = w[:, cols(0)], w[:, cols(RW)], w[:, cols(2 * RW)], w[:, cols(3 * RW)]
    fq, fk, fv = w[:, cols(4 * RW)], w[:, cols(5 * RW)], w[:, cols(6 * RW)]
    fl = w[:, 7 * RW + c:7 * RW + c + 1]

    def fm(m):
        return m.reshape(KC, 128, m.shape[1]).transpose(1, 0, 2)
    wf = np.ascontiguousarray(np.stack([fm(rq), fm(rk), fm(rg), fm(fq), fm(fk)], axis=1))
    wt = np.ascontiguousarray(fm(np.concatenate([rk, rv, fv, fl], axis=1)))
    kw, cdv, qw, dmat = ret_consts(c)
    return {
        "wf": wf, "wt": wt,
        "gmix": np.ascontiguousarray(norm_mix0.reshape(KC, 128).T),
        "fb": np.full((128, 1), fox_fb[c], np.float32),
        "gn": np.ascontiguousarray(ret_gn[c * 128:(c + 1) * 128].reshape(128, 1)),
        "kw": kw, "cd": cdv, "qw": qw, "dmat": dmat,
        "mu": causal_mu(), "tri": tri_consts(),
    }


BIGM = 30000.0
SELBIG = 1.0e9
LAMBDA_INIT1 = 0.8 - 0.6 * math.exp(-0.3 * 1)


def build_mult_tiles(kb, G_d, mask_d, negb, dnegb, dst, ddst, ntiles, stg_ring, tmp_ring):
    for t in range(ntiles):
        g_, dg_ = stg_ring.get()
        kb.dma("sp", g_[:], G_d[:, t, :], writes=[dg_])
        m_, dm_ = tmp_ring.get()
        kb.dma("act", m_[:], mask_d[:, t, :], writes=[dm_])
        kb.op("act", lambda e: e.activation(g_[:], g_[:], AF.Exp, bias=negb, scale=1.0),
              reads=[dg_, dnegb], writes=[dg_])
        kb.op("dve", lambda e: e.tensor_tensor(dst[:, t, :], g_[:], m_[:], op=ALU.mult),
              reads=[dg_, dm_], writes=[ddst])


def build_phase_a1(S):
    NT = S // 128
    NG = S // 512
    NS = S // 64
    NSR = min(128, NS)
    NST = (NS + 127) // 128
    NC = (S - 32) // 16 + 1
    NCT = (NC + 127) // 128
    NF = 13
    nc = bass.Bass("TRN2", target_bir_lowering=False)
    xT = nc.dram_tensor("xT", [D, S], F32, kind="ExternalInput").ap()
    wf_d = nc.dram_tensor("wf", [128, NF, KC, 128], F32, kind="ExternalInput").ap()
    wt_d = nc.dram_tensor("wt", [128, KC, 384], F32, kind="ExternalInput").ap()
    gmix_d = nc.dram_tensor("gmix", [128, KC], F32, kind="ExternalInput").ap()
    lam_d = nc.dram_tensor("lam", [128, 256], F32, kind="ExternalInput").ap()
    subln_d = nc.dram_tensor("subln", [128, 1], F32, kind="ExternalInput").ap()
    b31_d = nc.dram_tensor("b31", [128, 6], F32, kind="ExternalInput").ap()
    tg_diff_d = nc.dram_tensor("tg_diff", [128, 5, 512], F32, kind="ExternalInput").ap()
    tg_nsa_d = nc.dram_tensor("tg_nsa", [128, 5, 512], F32, kind="ExternalInput").ap()
    tm_d = nc.dram_tensor("tm", [128, 5, 512], BF16, kind="ExternalInput").ap()
    tg_cmp_d = nc.dram_tensor("tg_cmp", [128, 20, 512], F32, kind="ExternalInput").ap()
    tm_cmp_d = nc.dram_tensor("tm_cmp", [128, 5, 512], BF16, kind="ExternalInput").ap()
    wlow_d = nc.dram_tensor("wlow", [128, 4, 512], BF16, kind="ExternalInput").ap()
    ovx_d = nc.dram_tensor("ovx", [128, NCT, NS + 1], BF16, kind="ExternalInput").ap()
    exs_d = nc.dram_tensor("exs", [128, 64, 128], BF16, kind="ExternalInput").ap()
    ident_d = nc.dram_tensor("ident", [128, 128], BF16, kind="ExternalInput").ap()
    cw1_d = nc.dram_tensor("cw1", [2, 128, 32, 128], F32, kind="ExternalInput").ap()
    cw2_d = nc.dram_tensor("cw2", [2, 128, 128], F32, kind="ExternalInput").ap()
    cpos_d = nc.dram_tensor("cpos", [2, 128, 32], F32, kind="ExternalInput").ap()
    mixT = nc.dram_tensor("mixT", [256, S], BF16, kind="ExternalOutput").ap()
    fnames = ["dqT", "dkT", "nq0T", "nq1T", "nq2T", "nq3T", "ckT", "cvT", "skT", "wkT", "g0T", "g1T", "g2T"]
    fT = [nc.dram_tensor("s_" + n, [128, S], BF16, kind="Internal").ap() for n in fnames]
    dv_d = nc.dram_tensor("s_dv", [S, 128], BF16, kind="Internal").ap()
    sv_d = nc.dram_tensor("s_sv", [S, 128], BF16, kind="Internal").ap()
    wv_d = nc.dram_tensor("s_wv", [S, 128], BF16, kind="Internal").ap()
    o1_d = nc.dram_tensor("s_o1", [128, S], F32, kind="Internal").ap()
    ocmp_d = nc.dram_tensor("s_ocmp", [128, S], F32, kind="Internal").ap()
    oslc_d = nc.dram_tensor("s_oslc", [128, S], F32, kind="Internal").ap()
    selT_d = nc.dram_tensor("s_selT", [NST, 128, S], BF16, kind="Internal").ap()

    with ExitStack() as st:
        kb = KB(nc, st)
        pp = PsumPool(kb)
        outdeps = []
        ones = kb.sb([128, 128], BF16, name="ones")
        dones = Dep()
        kb.op("pool", lambda e: e.memset(ones[:], 1.0), writes=[dones])
        onesd = kb.sb([128, 128], BF16, name="onesd")
        dond = Dep()
        kb.op("pool", lambda e: e.memset(onesd[:], 1.0 / 128.0), writes=[dond])

        def small(name, shape, dt, src, q="sp"):
            t = kb.sb(shape, dt, name=name)
            d = Dep()
            kb.dma(q, t[:], src, writes=[d])
            return t, d
        gmix, dgm = small("gmix", [128, KC], F32, gmix_d)
        lam, dlam = small("lam", [128, 256], F32, lam_d)
        subln, dsub = small("subln", [128, 1], F32, subln_d)
        b31, db31 = small("b31", [128, 6], F32, b31_d)
        ident, did = small("ident", [128, 128], BF16, ident_d)
        negb = kb.sb([128, 6], F32, name="negb")
        dnegb = Dep()
        kb.op("dve", lambda e: e.tensor_scalar(negb[:], b31[:], -1.0, None, op0=ALU.mult), reads=[db31], writes=[dnegb])
        lt = kb.sb([128, 128], F32, name="lt")
        dlt = Dep()
        l2 = kb.sb([128, 4], F32, name="l2")
        dl2 = Dep()
        kb.op("dve", lambda e: e.tensor_tensor(lt[:, 0:64], lam[:, 0:64], lam[:, 64:128], op=ALU.mult), reads=[dlam], writes=[dlt])
        kb.op("dve", lambda e: e.tensor_tensor(lt[:, 64:128], lam[:, 128:192], lam[:, 192:256], op=ALU.mult), reads=[dlam, dlt], writes=[dlt])
        kb.op("dve", lambda e: e.tensor_reduce(l2[:, 0:1], lt[:, 0:64], axis=AX.X, op=ALU.add), reads=[dlt], writes=[dl2])
        kb.op("dve", lambda e: e.tensor_reduce(l2[:, 1:2], lt[:, 64:128], axis=AX.X, op=ALU.add), reads=[dlt, dl2], writes=[dl2])
        kb.op("act", lambda e: e.activation(l2[:, 0:2], l2[:, 0:2], AF.Exp), reads=[dl2], writes=[dl2])
        kb.op("dve", lambda e: e.tensor_tensor(l2[:, 2:3], l2[:, 1:2], l2[:, 0:1], op=ALU.subtract), reads=[dl2], writes=[dl2])
        kb.op("dve", lambda e: e.tensor_scalar(l2[:, 3:4], l2[:, 2:3], -LAMBDA_INIT1, None, op0=ALU.add), reads=[dl2], writes=[dl2])
        neglam = l2[:, 3:4]
        sgain = kb.sb([128, 1], F32, name="sgain")
        dsg_ = Dep()
        kb.op("dve", lambda e: e.tensor_scalar(sgain[:], subln[:], 1.0 - LAMBDA_INIT1, None, op0=ALU.mult), reads=[dsub], writes=[dsg_])

        with ExitStack() as st1:
            kb.stack = st1
            wl = WLoader(kb, nstage=2)
            wf = kb.sb([128, NF, KC, 128], BF16, name="wf")
            dwf = Dep()
            wt = kb.sb([128, KC, 384], BF16, name="wt")
            dwt = Dep()
            for gi in range(NF):
                load_weights_scaled(kb, wl, wf[:, gi], dwf, wf_d[:, gi], gmix, dgm, 128)
            load_weights_scaled(kb, wl, wt, dwt, wt_d, gmix, dgm, 384)
            xring = Ring(kb, 8, [128, 512], F32, "xr")
            hring = Ring(kb, 2, [128, KC, 512], BF16, "hr")
            sqring = Ring(kb, 16, [128, 512], BF16, "sqr")
            tdeps = {n: [Dep() for _ in range(NT)] for n in ("dv", "sv", "wv")}
            t_out = [(0, 128, dv_d, tdeps["dv"], None, None), (128, 128, sv_d, tdeps["sv"], None, None),
                     (256, 128, wv_d, tdeps["wv"], None, None)]
            fdeps = projection_pass(kb, pp, S, xT, wf, dwf, NF, wt, dwt, 384, fT, t_out, ones, dones,
                                    xring=xring, hring=hring, sqring=sqring)
            kb.barrier()
        kb.stack = st

        def load_kv(kT, dkT, vv, dvv, fk, fkd, vd, vdeps):
            for g in range(NG):
                kb.dma("sp", kT[:, g * 512:(g + 1) * 512], fk[:, g * 512:(g + 1) * 512], reads=[fkd[g]], writes=[dkT])
                kb.dma("act", vv[:, g * 4:(g + 1) * 4, :],
                       vd[g * 512:(g + 1) * 512, :].rearrange("(a p) b -> p a b", p=128),
                       reads=vdeps[g * 4:g * 4 + 4], writes=[dvv])

        def t5_mult(M, dM):
            def f(i, j):
                u = i - 4 * j
                if u >= -1:
                    return M[:, u + 1, :], dM
                return None
            return f

        with ExitStack() as st2:
            kb.stack = st2
            kT = kb.sb([128, S], BF16, name="kT")
            dkT = Dep()
            vv = kb.sb([128, NT, 128], BF16, name="vv")
            dvv = Dep()
            load_kv(kT, dkT, vv, dvv, fT[1], fdeps[1], dv_d, tdeps["dv"])
            stg = Ring(kb, 2, [128, 512], F32, "stg")
            tmpb = Ring(kb, 2, [128, 512], BF16, "tmpb")
            Md = kb.sb([128, 5, 512], BF16, name="Md")
            dMd = Dep()
            build_mult_tiles(kb, tg_diff_d, tm_d, negb[:, 0:1], dnegb, Md, dMd, 5, stg, tmpb)
            f32r = Ring(kb, 4, [128, 512], F32, "d_f32")
            b16r = Ring(kb, 3, [128, 512], BF16, "d_b16")
            o1deps = [Dep() for _ in range(NG)]

            def out0(j, Ops, dO, Lps, dL):
                r, dr = f32r.get()
                kb.op("dve", lambda e: e.reciprocal(r[:], Lps[:]), reads=[dL], writes=[dr])
                o, do_ = f32r.get()
                kb.op("dve", lambda e: e.tensor_tensor(o[:], Ops[:], r[:], op=ALU.mult), reads=[dO, dr], writes=[do_])
                kb.dma("sp", o1_d[:, j * 512:(j + 1) * 512], o[:], reads=[do_], writes=[o1deps[j]])

            def out1(j, Ops, dO, Lps, dL):
                r, dr = f32r.get()
                kb.op("dve", lambda e: e.reciprocal(r[:], Lps[:]), reads=[dL], writes=[dr])
                o2, do2 = f32r.get()
                kb.op("dve", lambda e: e.tensor_tensor(o2[:], Ops[:], r[:], op=ALU.mult), reads=[dO, dr], writes=[do2])
                o1, do1 = f32r.get()
                kb.dma("sp", o1[:], o1_d[:, j * 512:(j + 1) * 512], reads=[o1deps[j]], writes=[do1])
                kb.op("dve", lambda e: e.scalar_tensor_tensor(o2[:], o2[:], neglam, o1[:], op0=ALU.mult, op1=ALU.add),
                      reads=[do2, dl2, do1], writes=[do2])
                sq, dsq = b16r.get()
                kb.op("act", lambda e: e.activation(sq[:], o2[:], AF.Square), reads=[do2], writes=[dsq])
                pm, dpm = pp.sub([7]).get()
                kb.op("pe", lambda e: e.matmul(pm[:], onesd[:], sq[:], start=True, stop=True), reads=[dond, dsq], writes=[dpm])
                rstd_from_ss(kb, pm[:], dpm, r[:], dr, 512, inv_d=1.0)
                ob, dob = b16r.get()
                kb.op("dve", lambda e: e.scalar_tensor_tensor(ob[:], o2[:], sgain[:], r[:], op0=ALU.mult, op1=ALU.mult),
                      reads=[do2, dsg_, dr], writes=[dob])
                do = Dep()
                kb.dma("sp", mixT[0:128, j * 512:(j + 1) * 512], ob[:], reads=[dob], writes=[do])
                outdeps.append(do)

            for m, ofn in ((0, out0), (1, out1)):
                attention_pass(kb, pp, S, fT[0], fdeps[0], 64 * m, 64, kT, dkT, vv, dvv, 64 ** -0.5,
                               lambda j: (b31[:, 0:1], db31), t5_mult(Md, dMd),
                               lambda j: list(range(0, 4 * j + 4)), ofn, ones, dones, name="df%d" % m)
            kb.barrier()
        kb.stack = st

        with ExitStack() as st3:
            kb.stack = st3
            kcT = kb.sb([128, NCT * 128], BF16, name="kcT")
            dkc = Dep()
            vc = kb.sb([128, NCT, 128], BF16, name="vc")
            dvc = Dep()
            kb.op("pool", lambda e: e.memset(kcT[:], 0.0), writes=[dkc])
            kb.op("pool", lambda e: e.memset(vc[:], 0.0), writes=[dvc])
            with ExitStack() as st3a:
                kb.stack = st3a
                wl = WLoader(kb, nstage=2)
                kvT = kb.sb([128, S], BF16, name="kvT")
                dkv = Dep()
                w1b = kb.sb([128, 32, 128], BF16, name="w1b")
                dw1 = Dep()
                w2b = kb.sb([128, 128], BF16, name="w2b")
                dw2 = Dep()
                posb = kb.sb([128, 32], BF16, name="posb")
                dpos = Dep()
                bcol = kb.sb([128, 1], F32, name="bcol")
                dbc = Dep()
                hidr = Ring(kb, 2, [128, 512], BF16, "hid")
                for kv in range(2):
                    src = fT[6 + kv]
                    for g in range(NG):
                        kb.dma("sp", kvT[:, g * 512:(g + 1) * 512], src[:, g * 512:(g + 1) * 512],
                               reads=[fdeps[6 + kv][g]], writes=[dkv])
                    for l0 in range(0, 32, 16):
                        wl.load(w1b[:, l0:l0 + 16, :].rearrange("p a b -> p (a b)"), dw1,
                                cw1_d[kv, :, l0:l0 + 16, :].rearrange("p a b -> p (a b)"), 2048)
                    wl.load(w2b[:], dw2, cw2_d[kv], 128)
                    wl.load(posb[:], dpos, cpos_d[kv], 32)
                    pb_, dpb_ = pp.get()
                    for l in range(32):
                        kb.op("pe", lambda e: e.matmul(pb_[:, 0:1], w1b[:, l, :], posb[:, l:l + 1],
                                                       start=(l == 0), stop=(l == 31)),
                              reads=[dw1, dpos], writes=[dpb_], inc=(l == 31))
                    kb.op("dve", lambda e: e.tensor_copy(bcol[:], pb_[:, 0:1]), reads=[dpb_], writes=[dbc])
                    for n0 in range(0, NC, 512):
                        nn = min(512, NC - n0)
                        ph, dph = pp.get()
                        for l in range(32):
                            a0 = 16 * n0 + l
                            kb.op("pe", lambda e: e.matmul(ph[:, 0:nn], w1b[:, l, :], kvT[:, a0:a0 + 16 * (nn - 1) + 1:16],
                                                           start=(l == 0), stop=(l == 31)),
                                  reads=[dw1, dkv], writes=[dph], inc=(l == 31))
                        hid, dhid = hidr.get()
                        kb.op("act", lambda e: e.activation(hid[:, 0:nn], ph[:, 0:nn], AF.Silu, bias=bcol[:], scale=1.0),
                              reads=[dph, dbc], writes=[dhid])
                        if kv == 0:
                            pk, dpk = pp.get()
                            kb.op("pe", lambda e: e.matmul(pk[:, 0:nn], w2b[:], hid[:, 0:nn], start=True, stop=True),
                                  reads=[dw2, dhid], writes=[dpk])
                            kb.op("dve", lambda e: e.tensor_copy(kcT[:, n0:n0 + nn], pk[:, 0:nn]), reads=[dpk], writes=[dkc])
                        else:
                            for b0 in range(0, nn, 128):
                                bn = min(128, nn - b0)
                                pv_, dpv_ = pp.get()
                                kb.op("pe", lambda e: e.matmul(pv_[0:bn, 0:128], hid[:, b0:b0 + bn], w2b[:], start=True, stop=True),
                                      reads=[dw2, dhid], writes=[dpv_])
                                kb.op("dve", lambda e: e.tensor_copy(vc[0:bn, (n0 + b0) // 128, :], pv_[0:bn, 0:128]),
                                      reads=[dpv_], writes=[dvc])
                kb.barrier()
            kb.stack = st3
            stg = Ring(kb, 2, [128, 512], F32, "stg3")
            tmpb = Ring(kb, 2, [128, 512], BF16, "tmpb3")
            Mc = kb.sb([128, 20, 512], BF16, name="Mc")
            dMc = Dep()
            for hh in range(4):
                build_mult_tiles(kb, tg_cmp_d[:, hh * 5:(hh + 1) * 5, :], tm_cmp_d, negb[:, 2 + hh:3 + hh], dnegb,
                                 Mc[:, hh * 5:(hh + 1) * 5, :], dMc, 5, stg, tmpb)
            ovx, dovx = small("ovx", [128, NCT, NS + 1], BF16, ovx_d)
            qring = Ring(kb, 3, [128, 512], BF16, "cq")
            PT = [Ring(kb, NCT, [128, 512], BF16, "cp%d" % k) for k in range(2)]
            pe_ring = Ring(kb, 3, [128, 512], BF16, "cpe")
            imp = kb.sb([128, 4, NS], F32, name="imp")
            dimp = Dep()
            rl = Ring(kb, 3, [128, 1], F32, "crl")
            m8 = Ring(kb, 4, [128, 8], F32, "cm8")
            wk_ = Ring(kb, 2, [128, NS], F32, "cwk")
            selb = Ring(kb, 2, [128, NS], BF16, "csel")
            selT = Ring(kb, 2, [128, NST, 512], BF16, "cselT")
            f32r = Ring(kb, 3, [128, 512], F32, "c_f32")
            gring = Ring(kb, 2, [128, 512], BF16, "c_g")
            ocdeps = [Dep() for _ in range(NG)]
            sdeps = [Dep() for _ in range(NG)]
            pacc = pp.sub([0, 1])
            pst = pp.sub([2, 3, 4])
            pg_ = pp.sub([5, 6])
            ptr = pp.sub([7])
            citems = []
            for j in range(NG):
                nmax = min(NC - 1, (512 * j + 480) // 16)
                its = list(range(0, nmax // 128 + 1))
                for hh in range(4):
                    for n_i, i in enumerate(its):
                        citems.append((j, hh, n_i, i, len(its)))
            cst = {}
            cq = {}
            cacc = {}

            def c_issue(t):
                j, hh, n_i, i, nk = citems[t]
                if n_i == 0:
                    qt, dqt = qring.get()
                    kb.dma("sp", qt[:], fT[2 + hh][:, j * 512:(j + 1) * 512], reads=[fdeps[2 + hh][j]], writes=[dqt])
                    cq[(j, hh)] = (qt, dqt)
                qt, dqt = cq[(j, hh)]
                sT, dsT = pst.get()
                kb.op("pe", lambda e: e.matmul(sT[:], kcT[:, i * 128:(i + 1) * 128], qt[:], start=True, stop=True),
                      reads=[dkc, dqt], writes=[dsT])
                o = j - 4 * i
                pT_, dpT_ = PT[hh % 2].tiles[i], PT[hh % 2].deps[i]
                if o <= 4:
                    pe_, dpe_ = pe_ring.get()
                    kb.op("act", lambda e: e.activation(pe_[:], sT[:], AF.Exp, bias=b31[:, 2 + hh:3 + hh], scale=128 ** -0.5),
                          reads=[dsT, db31], writes=[dpe_])
                    kb.op("dve", lambda e: e.tensor_tensor(pT_[:], pe_[:], Mc[:, hh * 5 + o, :], op=ALU.mult),
                          reads=[dpe_, dMc], writes=[dpT_])
                else:
                    kb.op("act", lambda e: e.activation(pT_[:], sT[:], AF.Exp, bias=b31[:, 2 + hh:3 + hh], scale=128 ** -0.5),
                          reads=[dsT, db31], writes=[dpT_])
                cst[t] = (pT_, dpT_)

            def c_post(t):
                j, hh, n_i, i, nk = citems[t]
                pT_, dpT_ = cst.pop(t)
                if hh == 0:
                    if n_i == 0:
                        cacc["O"], cacc["dO"] = pacc.get()
                        cacc["L"], cacc["dL"] = pacc.get()
                    first, last = (n_i == 0), (n_i == nk - 1)
                    kb.op("pe", lambda e: e.matmul(cacc["O"][:], vc[:, i, :], pT_[:], start=first, stop=last),
                          reads=[dvc, dpT_], writes=[cacc["dO"]])
                    kb.op("pe", lambda e: e.matmul(cacc["L"][:], ones[:], pT_[:], start=first, stop=last),
                          reads=[dones, dpT_], writes=[cacc["dL"]])
                if n_i != nk - 1:
                    return
                its = [citems[t - (nk - 1) + k][3] for k in range(nk)]
                for b in range(4):
                    G, dG = pg_.get()
                    for k_i, ii in enumerate(its):
                        pa, dpa_ = PT[hh % 2].tiles[ii], PT[hh % 2].deps[ii]
                        kb.op("pe", lambda e: e.matmul(G[:, 0:NS + 1], pa[:, b * 128:(b + 1) * 128], ovx[:, ii, :],
                                                       start=(k_i == 0), stop=(k_i == nk - 1)),
                              reads=[dpa_, dovx], writes=[dG])
                    r_, dr_ = rl.get()
                    kb.op("dve", lambda e: e.tensor_scalar(r_[:], G[:, NS:NS + 1], 1e-30, None, op0=ALU.max),
                          reads=[dG], writes=[dr_])
                    kb.op("dve", lambda e: e.reciprocal(r_[:], r_[:]), reads=[dr_], writes=[dr_])
                    if hh == 0:
                        kb.op("dve", lambda e: e.tensor_scalar(imp[:, b, :], G[:, 0:NS], r_[:], None, op0=ALU.mult),
                              reads=[dG, dr_], writes=[dimp])
                    else:
                        kb.op("dve", lambda e: e.scalar_tensor_tensor(imp[:, b, :], G[:, 0:NS], r_[:], imp[:, b, :],
                                                                      op0=ALU.mult, op1=ALU.add),
                              reads=[dG, dr_, dimp], writes=[dimp])
                if hh == 0:
                    Ops, dO, Lps, dL = cacc["O"], cacc["dO"], cacc["L"], cacc["dL"]
                    r, dr = f32r.get()
                    kb.op("dve", lambda e: e.tensor_scalar(r[:], Lps[:], 1e-30, None, op0=ALU.max), reads=[dL], writes=[dr])
                    kb.op("dve", lambda e: e.reciprocal(r[:], r[:]), reads=[dr], writes=[dr])
                    oc, doc = f32r.get()
                    kb.op("dve", lambda e: e.tensor_tensor(oc[:], Ops[:], r[:], op=ALU.mult), reads=[dO, dr], writes=[doc])
                    gt, dgt = gring.get()
                    kb.dma("sp", gt[:], fT[10][:, j * 512:(j + 1) * 512], reads=[fdeps[10][j]], writes=[dgt])
                    kb.op("act", lambda e: e.activation(r[:], gt[:], AF.Sigmoid), reads=[dgt, dr], writes=[dr])
                    kb.op("dve", lambda e: e.tensor_tensor(oc[:], oc[:], r[:], op=ALU.mult), reads=[doc, dr], writes=[doc])
                    kb.dma("sp", ocmp_d[:, j * 512:(j + 1) * 512], oc[:], reads=[doc], writes=[ocdeps[j]])
                if hh != 3:
                    return
                sT_, dsT_ = selT.get()
                for b in range(4):
                    B = 4 * j + b
                    sc = imp[:, b, :]
                    if 2 * B + 1 < NS:
                        kb.op("dve", lambda e: e.memset(imp[0:64, b, 2 * B + 1:NS], -SELBIG), reads=[dimp], writes=[dimp])
                    if 2 * B + 2 < NS:
                        kb.op("dve", lambda e: e.memset(imp[64:128, b, 2 * B + 2:NS], -SELBIG), reads=[dimp], writes=[dimp])
                    kb.op("dve", lambda e: e.memset(imp[:, b, 0:1], SELBIG), reads=[dimp], writes=[dimp])
                    kb.op("dve", lambda e: e.memset(imp[0:64, b, max(2 * B - 1, 0):2 * B + 1], SELBIG), reads=[dimp], writes=[dimp])
                    kb.op("dve", lambda e: e.memset(imp[64:128, b, 2 * B:2 * B + 2], SELBIG), reads=[dimp], writes=[dimp])
                    m1, dm1 = m8.get()
                    kb.op("dve", lambda e: e.max(m1[:], sc), reads=[dimp], writes=[dm1])
                    w_, dw_ = wk_.get()
                    kb.op("dve", lambda e: e.match_replace(w_[:], m1[:], sc, -3.0e9), reads=[dimp, dm1], writes=[dw_])
                    m2, dm2 = m8.get()
                    kb.op("dve", lambda e: e.max(m2[:], w_[:]), reads=[dw_], writes=[dm2])
                    sb_, dsb_ = selb.get()
                    kb.op("dve", lambda e: e.tensor_scalar(sb_[:], sc, m2[:, 7:8], 1.0, op0=ALU.is_ge, op1=ALU.subtract),
                          reads=[dimp, dm2], writes=[dsb_])
                    for jt in range(NST):
                        tp, dtp = ptr.get()
                        tpb = tp[:].bitcast(BF16)
                        kb.op("pe", lambda e: e.transpose(tpb[0:NSR, 0:128], sb_[:, jt * 128:jt * 128 + NSR], ident[:]),
                              reads=[dsb_, did], writes=[dtp])
                        kb.op("act", lambda e: e.activation(sT_[0:NSR, jt, b * 128:(b + 1) * 128], tpb[0:NSR, 0:128], AF.Copy),
                              reads=[dtp], writes=[dsT_])
                for jt in range(NST):
                    kb.dma("sp", selT_d[jt, 0:NSR, j * 512:(j + 1) * 512], sT_[0:NSR, jt, :], reads=[dsT_], writes=[sdeps[j]])

            CLOOK = 2
            for t in range(len(citems) + CLOOK):
                if t - CLOOK >= 0:
                    c_post(t - CLOOK)
                if t < len(citems):
                    c_issue(t)
            kb.barrier()
        kb.stack = st

        with ExitStack() as st4:
            kb.stack = st4
            kT = kb.sb([128, S], BF16, name="kT4")
            dkT = Dep()
            vv = kb.sb([128, NT, 128], BF16, name="vv4")
            dvv = Dep()
            stg = Ring(kb, 2, [128, 512], F32, "stg4")
            tmpb = Ring(kb, 2, [128, 512], BF16, "tmpb4")
            Mn = kb.sb([128, 5, 512], BF16, name="Mn")
            dMn = Dep()
            build_mult_tiles(kb, tg_nsa_d, tm_d, negb[:, 1:2], dnegb, Mn, dMn, 5, stg, tmpb)
            exs, dexs = small("exs", [128, 64, 128], BF16, exs_d)
            wlow, dwlow = small("wlow", [128, 4, 512], BF16, wlow_d, q="act")
            Mw = kb.sb([128, 512], BF16, name="Mw")
            dMw = Dep()
            kb.op("dve", lambda e: e.tensor_tensor(Mw[:], Mn[:, 0, :], wlow[:, 3, :], op=ALU.mult), reads=[dMn, dwlow], writes=[dMw])
            selr = Ring(kb, 2, [128, NST, 512], BF16, "selr")
            f32r = Ring(kb, 5, [128, 512], F32, "s_f32")
            gring = Ring(kb, 2, [128, 512], BF16, "s_g")
            b16r = Ring(kb, 2, [128, 512], BF16, "s_b16")
            osdeps = [Dep() for _ in range(NG)]
            cur_sel = {}

            def pre_slc(j):
                t_, d_ = selr.get()
                for jt in range(NST):
                    kb.dma("sp", t_[0:NSR, jt, :], selT_d[jt, 0:NSR, j * 512:(j + 1) * 512], reads=[sdeps[j]], writes=[d_])
                cur_sel["t"] = (t_, d_)

            def extra_slc(i, j):
                t_, d_ = cur_sel["t"]
                jt = (2 * i) // 128
                k = ((2 * i) % 128) // 2
                return exs[0:NSR, k, :], t_[0:NSR, jt, :], [dexs, d_]

            def gated(j, Ops, dO, Lps, dL, gidx):
                r, dr = f32r.get()
                kb.op("dve", lambda e: e.reciprocal(r[:], Lps[:]), reads=[dL], writes=[dr])
                o, do_ = f32r.get()
                kb.op("dve", lambda e: e.tensor_tensor(o[:], Ops[:], r[:], op=ALU.mult), reads=[dO, dr], writes=[do_])
                gt, dgt = gring.get()
                kb.dma("act", gt[:], fT[gidx][:, j * 512:(j + 1) * 512], reads=[fdeps[gidx][j]], writes=[dgt])
                kb.op("act", lambda e: e.activation(r[:], gt[:], AF.Sigmoid), reads=[dgt, dr], writes=[dr])
                kb.op("dve", lambda e: e.tensor_tensor(o[:], o[:], r[:], op=ALU.mult), reads=[do_, dr], writes=[do_])
                return o, do_

            def out_slc(j, Ops, dO, Lps, dL):
                o, do_ = gated(j, Ops, dO, Lps, dL, 11)
                kb.dma("sp", oslc_d[:, j * 512:(j + 1) * 512], o[:], reads=[do_], writes=[osdeps[j]])

            def out_win(j, Ops, dO, Lps, dL):
                o, do_ = gated(j, Ops, dO, Lps, dL, 12)
                for src, dl in ((ocmp_d, ocdeps), (oslc_d, osdeps)):
                    a, da_ = f32r.get()
                    kb.dma("sp", a[:], src[:, j * 512:(j + 1) * 512], reads=[dl[j]], writes=[da_])
                    kb.op("dve", lambda e: e.tensor_tensor(o[:], o[:], a[:], op=ALU.add), reads=[do_, da_], writes=[do_])
                ob, dob = b16r.get()
                kb.op("act", lambda e: e.activation(ob[:], o[:], AF.Copy), reads=[do_], writes=[dob])
                do = Dep()
                kb.dma("sp", mixT[128:256, j * 512:(j + 1) * 512], ob[:], reads=[dob], writes=[do])
                outdeps.append(do)

            def mult_win(i, j):
                u = i - 4 * j
                if u >= 0:
                    return Mn[:, u + 1, :], dMn
                if u == -1:
                    return Mw[:], dMw
                return wlow[:, u + 4, :], dwlow

            load_kv(kT, dkT, vv, dvv, fT[8], fdeps[8], sv_d, tdeps["sv"])
            attention_pass(kb, pp, S, fT[2], fdeps[2], 0, 128, kT, dkT, vv, dvv, 128 ** -0.5,
                           lambda j: (b31[:, 1:2], db31), t5_mult(Mn, dMn),
                           lambda j: list(range(0, 4 * j + 4)), out_slc, ones, dones, extra_fn=extra_slc,
                           pre_fn=pre_slc, name="sl")
            load_kv(kT, dkT, vv, dvv, fT[9], fdeps[9], wv_d, tdeps["wv"])
            attention_pass(kb, pp, S, fT[2], fdeps[2], 0, 128, kT, dkT, vv, dvv, 128 ** -0.5,
                           lambda j: (b31[:, 1:2], db31), mult_win,
                           lambda j: list(range(max(0, 4 * j - 4), 4 * j + 4)), out_win, ones, dones, name="wn")
        kb.stack = st
        kb.finish(outdeps)
    return nc


def t5_bucket_np(dist):
    n = np.maximum(dist, 0)
    nf = np.maximum(n, 1).astype(np.float32)
    large = 16 + (np.log(nf / np.float32(16)) / np.float32(math.log(128 / 16)) * np.float32(16)).astype(np.int32)
    large = np.minimum(large, 31)
    return np.where(n < 16, n, large)


def t5_tiles(table_col):
    s = np.arange(128)[:, None]
    q = np.arange(512)[None, :]
    G = np.zeros((128, 5, 512), np.float32)
    M = np.zeros((128, 5, 512), np.float32)
    for ui, u in enumerate(range(-1, 4)):
        dist = (q // 128 - u) * 128 + (q % 128) - s
        valid = dist >= 0
        G[:, ui, :] = np.where(valid, table_col[t5_bucket_np(dist)], table_col[31])
        M[:, ui, :] = valid
    return G, M.astype(ml_dtypes.bfloat16)


def cmp_tiles(table_col):
    n = np.arange(128)[:, None]
    q = np.arange(512)[None, :]
    G = np.zeros((128, 5, 512), np.float32)
    M = np.zeros((128, 5, 512), np.float32)
    for o in range(5):
        dist = 512 * o + q - 16 * n - 31
        valid = dist >= 0
        G[:, o, :] = np.where(valid, table_col[t5_bucket_np(dist)], table_col[31])
        M[:, o, :] = valid
    return G, M.astype(ml_dtypes.bfloat16)


def win_low_tiles():
    s = np.arange(128)[:, None]
    q = np.arange(512)[None, :]
    out = np.zeros((128, 4, 512), np.float32)
    for up in range(4):
        delta = q // 128 + 4 - up
        out[:, up, :] = np.where(delta > 4, 0.0, np.where(delta == 4, ((q % 128) < s) * 1.0, 1.0))
    return out.astype(ml_dtypes.bfloat16)


def a1_static(S):
    NS = S // 64
    NC = (S - 32) // 16 + 1
    NCT = (NC + 127) // 128
    n = np.arange(NCT * 128)
    jj = np.arange(NS)
    ov = ((16 * n[:, None] <= 64 * jj[None, :] + 63) & (16 * n[:, None] + 31 >= 64 * jj[None, :]) & (n[:, None] < NC))
    ovx = np.concatenate([ov.astype(np.float32), np.ones((NCT * 128, 1), np.float32)], axis=1)
    ovx = np.ascontiguousarray(ovx.reshape(NCT, 128, NS + 1).transpose(1, 0, 2)).astype(ml_dtypes.bfloat16)
    exs = np.zeros((128, 64, 128), np.float32)
    for k in range(64):
        exs[2 * k, k, 0:64] = BIGM
        exs[2 * k + 1, k, 64:128] = BIGM
    return {"ovx": ovx, "exs": exs.astype(ml_dtypes.bfloat16), "ident": np.eye(128, dtype=np.float32).astype(ml_dtypes.bfloat16),
            "wlow": win_low_tiles()}


def prep_phase_a1(c, S, w_in_odd, norm_mix1, diff_lambda, diff_subln, cmp_pos, cmp_w1, cmp_w2, t5_table, static=None):
    g, r = c // 4, c % 4
    order = [r] + [k for k in range(4) if k != r]
    w = w_in_odd

    def col(base, idx, n=128):
        return w[:, base + idx * n: base + (idx + 1) * n]

    def fm(m):
        return m.reshape(KC, 128, m.shape[1]).transpose(1, 0, 2)
    groups = [col(0, c), col(1024, c)]
    groups += [w[:, 3072 + g * 512 + k * 128: 3072 + g * 512 + (k + 1) * 128] for k in order]
    groups += [col(4096, g), col(4352, g), col(4608, g), col(5120, g)]
    for jb in range(3):
        cc = 5632 + jb * 8 + c
        groups.append(np.repeat(w[:, cc:cc + 1], 128, axis=1))
    wf = np.ascontiguousarray(np.stack([fm(m) for m in groups], axis=1))
    wt = np.ascontiguousarray(fm(np.concatenate([col(2048, c), col(4864, g), col(5376, g)], axis=1)))
    heads_cols = [c, 8 + c] + [8 + g * 4 + k for k in order]
    b31 = np.ascontiguousarray(np.tile(t5_table[31, heads_cols][None, :], (128, 1)))
    tg_diff, tm = t5_tiles(t5_table[:, c])
    tg_nsa, _ = t5_tiles(t5_table[:, 8 + c])
    tgc = []
    for k in order:
        gk, tmc = cmp_tiles(t5_table[:, 8 + g * 4 + k])
        tgc.append(gk)
    d = dict(static if static is not None else a1_static(S))
    d.update({
        "wf": wf, "wt": wt,
        "gmix": np.ascontiguousarray(norm_mix1.reshape(KC, 128).T),
        "lam": np.ascontiguousarray(np.tile(diff_lambda.reshape(1, 256), (128, 1))),
        "subln": np.ascontiguousarray(diff_subln.reshape(128, 1)),
        "b31": b31, "tg_diff": tg_diff, "tg_nsa": tg_nsa, "tm": tm,
        "tg_cmp": np.ascontiguousarray(np.concatenate(tgc, axis=1)), "tm_cmp": tmc,
        "cw1": np.ascontiguousarray(cmp_w1.reshape(2, 32, 128, 128).transpose(0, 2, 1, 3)),
        "cw2": np.ascontiguousarray(cmp_w2),
        "cpos": np.ascontiguousarray(cmp_pos.transpose(0, 2, 1)),
    })
    return d


SEQ = 16384
_NC_CACHE = {}


def _get_nc(key, fn):
    if key not in _NC_CACHE:
        _NC_CACHE[key] = fn()
    return _NC_CACHE[key]


def _assemble_mix(results):
    S = results[0]["mixT"].shape[1]
    mixT = np.empty((D, S), dtype=ml_dtypes.bfloat16)
    for c in range(NCORES):
        m = np.asarray(results[c]["mixT"])
        mixT[c * 128:(c + 1) * 128] = m[0:128]
        mixT[1024 + c * 128:1024 + (c + 1) * 128] = m[128:256]
    return mixT


def _run_b(xT, mixT, pT, wd, final):
    S = xT.shape[1]
    T = S // NCORES
    nc = _get_nc(("b", T, final), lambda: build_phase_b(T, final))
    ims = []
    for c in range(NCORES):
        d = dict(wd)
        d["xT"] = np.ascontiguousarray(xT[:, c * T:(c + 1) * T])
        d["mixT"] = np.ascontiguousarray(mixT[:, c * T:(c + 1) * T])
        d["pT"] = np.ascontiguousarray(pT[:, c * T:(c + 1) * T])
        ims.append(d)
    res = run_bass_kernel_spmd(nc, ims, core_ids=list(range(NCORES)))
    return np.concatenate([np.asarray(res.results[c]["outT"]) for c in range(NCORES)], axis=1)


def kernel(x, p, norm_mix, norm_ffn, w_in_even, ret_gn, fox_fb, w_out_even, w_in_odd, diff_lambda, diff_subln,
           cmp_pos, cmp_w1, cmp_w2, w_out_odd, t5_table, ffn_gate, ffn_up, ffn_down, ple_gate, ple_proj, final_norm):
    f = lambda a: np.asarray(a, dtype=np.float32)
    x, p = f(x), f(p)
    S = x.shape[1]
    xT = np.ascontiguousarray(x[0].T)
    nc = _get_nc(("a0", S), lambda: build_phase_a0(S))
    ims = []
    for c in range(NCORES):
        d = prep_phase_a0(c, f(w_in_even)[0], f(norm_mix)[0], f(ret_gn)[0], f(fox_fb)[0])
        d["xT"] = xT
        ims.append(d)
    res = run_bass_kernel_spmd(nc, ims, core_ids=list(range(NCORES)))
    mixT = _assemble_mix(res.results)
    wd = prep_phase_b_weights(f(w_out_even)[0], f(ffn_gate)[0], f(ffn_up)[0], f(ffn_down)[0], f(ple_gate)[0],
                              f(ple_proj)[0], f(norm_ffn)[0], f(final_norm))
    xT = _run_b(xT, mixT, np.ascontiguousarray(p[0, 0].T), wd, False)
    nc = _get_nc(("a1", S), lambda: build_phase_a1(S))
    static = a1_static(S)
    ims = []
    for c in range(NCORES):
        d = prep_phase_a1(c, S, f(w_in_odd)[0], f(norm_mix)[1], f(diff_lambda)[0], f(diff_subln)[0], f(cmp_pos)[0],
                          f(cmp_w1)[0], f(cmp_w2)[0], f(t5_table), static)
        d["xT"] = xT
        ims.append(d)
    res = run_bass_kernel_spmd(nc, ims, core_ids=list(range(NCORES)))
    mixT = _assemble_mix(res.results)
    wd = prep_phase_b_weights(f(w_out_odd)[0], f(ffn_gate)[1], f(ffn_up)[1], f(ffn_down)[1], f(ple_gate)[1],
                              f(ple_proj)[1], f(norm_ffn)[1], f(final_norm))
    outT = _run_b(xT, mixT, np.ascontiguousarray(p[1, 0].T), wd, True)
    return np.ascontiguousarray(outT.T).reshape(1, S, D).astype(np.float32)
```
